# Optimizing a Trainium2 kernel written in Bass

```python
import jax, jax.numpy as jnp
from jax import lax
import numpy as np

D_MODEL = 1024
BATCH = 4
SEQ = 4096
DEPTH = 4

N_EVEN = (DEPTH + 1) // 2
N_ODD = DEPTH // 2

A_HEADS = 4
A_DK = 128
A_DV = 128
A_WIDTH = A_HEADS * A_DK
B_HEADS = 4
B_DIM = 128
B_WIDTH = B_HEADS * B_DIM
AB_IN = 4 * A_WIDTH + 3 * B_WIDTH
AB_MIX = A_WIDTH + B_WIDTH
HGRN_CHUNK = 64
SB_BLOCK = 128

C_FFN = 6 * D_MODEL
C_HALF = C_FFN // 2
C_GROUPS = 8
C_GROUP_DIM = C_HALF // C_GROUPS
C_CHUNK = 128

MLP_HIDDEN = 4 * D_MODEL
EPS = 1e-6

kernel_name = "hybrid_hgrn2_stickbreak_gmlp_trunk"


def rms_norm(x, g):
    xf = x.astype(jnp.float32)
    y = xf * lax.rsqrt(jnp.mean(xf * xf, axis=-1, keepdims=True) + EPS)
    return (y * g.astype(jnp.float32)).astype(x.dtype)


def layer_norm(x, g, b):
    xf = x.astype(jnp.float32)
    mu = jnp.mean(xf, axis=-1, keepdims=True)
    xc = xf - mu
    y = xc * lax.rsqrt(jnp.mean(xc * xc, axis=-1, keepdims=True) + EPS)
    return (y * g.astype(jnp.float32) + b.astype(jnp.float32)).astype(x.dtype)


def hgrn2_mix(q, f_logit, i, g, lb, g_norm):
    dt = q.dtype
    bsz, s_len, _ = q.shape
    n_chunks = s_len // HGRN_CHUNK
    f32 = jnp.float32
    lb = lb.astype(f32)
    f = lb + (1.0 - lb) * jax.nn.sigmoid(f_logit.astype(f32))
    log_f = jnp.log(f)
    k = 1.0 - f
    qf = jax.nn.silu(q.astype(f32))
    v = i.astype(f32)

    def to_chunks(t):
        return t.reshape(bsz, n_chunks, HGRN_CHUNK, A_HEADS, -1).transpose(1, 0, 3, 2, 4)

    qc, kc, vc, lfc = to_chunks(qf), to_chunks(k), to_chunks(v), to_chunks(log_f)
    tri = jnp.tril(jnp.ones((HGRN_CHUNK, HGRN_CHUNK), dtype=bool))

    def step(state, xs):
        qj, kj, vj, lfj = xs
        b = jnp.cumsum(lfj, axis=2)
        diff = b[:, :, :, None, :] - b[:, :, None, :, :]
        decay = jnp.exp(jnp.where(tri[:, :, None], diff, -jnp.inf))
        scores = jnp.einsum('bhtd,bhtsd,bhsd->bhts', qj, decay, kj)
        o = (jnp.einsum('bhts,bhsv->bhtv', scores, vj)
             + jnp.einsum('bhtd,bhdv->bhtv', qj * jnp.exp(b), state))
        b_last = b[:, :, -1:, :]
        state = (jnp.exp(b_last[:, :, 0, :])[..., None] * state
                 + jnp.einsum('bhsd,bhsv->bhdv', kj * jnp.exp(b_last - b), vj))
        return state, o

    state0 = jnp.zeros((bsz, A_HEADS, A_DK, A_DV), f32)
    _, o = lax.scan(step, state0, (qc, kc, vc, lfc))
    o = o.transpose(1, 0, 3, 2, 4).reshape(bsz, s_len, A_HEADS, A_DV)
    o = o * lax.rsqrt(jnp.mean(o * o, axis=-1, keepdims=True) + EPS)
    o = o.reshape(bsz, s_len, A_WIDTH) * g_norm.astype(f32) * jax.nn.silu(g.astype(f32))
    return o.astype(dt)


def stick_breaking_attention(q, k, v):
    dt = q.dtype
    bsz, s_len, _ = q.shape
    n_blocks = s_len // SB_BLOCK
    scale = B_DIM ** -0.5

    def heads(t):
        return t.reshape(bsz, s_len, B_HEADS, B_DIM).transpose(0, 2, 1, 3)

    qh, kh, vh = heads(q), heads(k), heads(v)
    q_blocks = qh.reshape(bsz, B_HEADS, n_blocks, SB_BLOCK, B_DIM).transpose(2, 0, 1, 3, 4)
    key_pos = jnp.arange(s_len)

    def one_block(args):
        qi, blk = args
        z = jnp.einsum('bhqd,bhkd->bhqk', qi, kh).astype(jnp.float32) * scale
        t_pos = blk * SB_BLOCK + jnp.arange(SB_BLOCK)
        causal = key_pos[None, :] < t_pos[:, None]
        log_beta = jax.nn.log_sigmoid(z)
        log_keep = jnp.where(causal, jax.nn.log_sigmoid(-z), 0.0)
        suffix = lax.cumsum(log_keep, axis=3, reverse=True) - log_keep
        attn = jnp.where(causal, jnp.exp(log_beta + suffix), 0.0)
        return jnp.einsum('bhqk,bhkd->bhqd', attn.astype(vh.dtype), vh)

    o = lax.map(one_block, (q_blocks, jnp.arange(n_blocks)))
    o = o.transpose(1, 0, 3, 2, 4).reshape(bsz, s_len, B_WIDTH)
    return o.astype(dt)


def chunked_gmlp(h, ln_g, ln_b, ws, bs):
    bsz, s_len, _ = h.shape
    z = jax.nn.gelu(h, approximate=False)
    u, v = z[..., :C_HALF], z[..., C_HALF:]
    v = layer_norm(v, ln_g, ln_b)
    v = v.reshape(bsz, s_len // C_CHUNK, C_CHUNK, C_GROUPS, C_GROUP_DIM)
    w = ws * jnp.tril(jnp.ones((C_CHUNK, C_CHUNK), dtype=ws.dtype))[None]
    mixed = jnp.einsum('gts,bnsgc->bntgc', w, v) + bs.T[None, None, :, :, None]
    return u * mixed.reshape(bsz, s_len, C_HALF)


def setup_inputs(seed: int = 0) -> dict:
    key = jax.random.key(seed)
    ks = jax.random.split(key, 16)
    nrm = jax.random.normal
    f32 = jnp.float32
    return {
        "x": nrm(ks[0], (BATCH, SEQ, D_MODEL), f32),
        "mix_norm": 1.0 + 0.02 * nrm(ks[1], (DEPTH, D_MODEL), f32),
        "mlp_norm": 1.0 + 0.02 * nrm(ks[2], (DEPTH, D_MODEL), f32),
        "mlp_w1": nrm(ks[3], (DEPTH, D_MODEL, MLP_HIDDEN), f32) * D_MODEL ** -0.5,
        "mlp_w2": nrm(ks[4], (DEPTH, MLP_HIDDEN, D_MODEL), f32) * MLP_HIDDEN ** -0.5,
        "ab_w_in": nrm(ks[5], (N_EVEN, D_MODEL, AB_IN), f32) * D_MODEL ** -0.5,
        "ab_w_out": nrm(ks[6], (N_EVEN, AB_MIX, D_MODEL), f32) * AB_MIX ** -0.5,
        "hgrn_lb_logits": nrm(ks[7], (N_EVEN, A_WIDTH), f32),
        "hgrn_out_norm": 1.0 + 0.02 * nrm(ks[8], (N_EVEN, A_WIDTH), f32),
        "gm_w_in": nrm(ks[9], (N_ODD, D_MODEL, C_FFN), f32) * D_MODEL ** -0.5,
        "gm_ln_g": 1.0 + 0.02 * nrm(ks[10], (N_ODD, C_HALF), f32),
        "gm_ln_b": 0.02 * nrm(ks[11], (N_ODD, C_HALF), f32),
        "gm_ws": nrm(ks[12], (N_ODD, C_GROUPS, C_CHUNK, C_CHUNK), f32) * C_CHUNK ** -0.5,
        "gm_bs": 1.0 + 0.02 * nrm(ks[13], (N_ODD, C_GROUPS, C_CHUNK), f32),
        "gm_w_out": nrm(ks[14], (N_ODD, C_HALF, D_MODEL), f32) * C_HALF ** -0.5,
        "final_norm": 1.0 + 0.02 * nrm(ks[15], (D_MODEL,), f32),
    }


def reference(x, mix_norm, mlp_norm, mlp_w1, mlp_w2, ab_w_in, ab_w_out,
              hgrn_lb_logits, hgrn_out_norm, gm_w_in, gm_ln_g, gm_ln_b, gm_ws,
              gm_bs, gm_w_out, final_norm):
    lb_cum = jnp.cumsum(jax.nn.softmax(hgrn_lb_logits.astype(jnp.float32), axis=0), axis=0)
    lower_bounds = lb_cum - lb_cum[0:1]
    splits = [int(s) for s in np.cumsum([A_WIDTH] * 4 + [B_WIDTH] * 3)[:-1]]

    for layer in range(DEPTH):
        h = rms_norm(x, mix_norm[layer])
        if layer % 2 == 0:
            e = layer // 2
            proj = h @ ab_w_in[e]
            qa, fa, ia, ga, qb, kb, vb = jnp.split(proj, splits, axis=-1)
            oa = hgrn2_mix(qa, fa, ia, ga, lower_bounds[e], hgrn_out_norm[e])
            ob = stick_breaking_attention(qb, kb, vb)
            mix = jnp.concatenate([oa, ob], axis=-1) @ ab_w_out[e]
        else:
            o = layer // 2
            gated = chunked_gmlp(h @ gm_w_in[o], gm_ln_g[o], gm_ln_b[o], gm_ws[o], gm_bs[o])
            mix = gated @ gm_w_out[o]
        x = x + mix
        h = rms_norm(x, mlp_norm[layer])
        x = x + jnp.square(jax.nn.relu(h @ mlp_w1[layer])) @ mlp_w2[layer]

    return rms_norm(x, final_norm)
```

```python
import contextlib
import numpy as np
import ml_dtypes
import concourse.bass as bass
import concourse.mybir as mybir
from concourse.bass_utils import run_bass_kernel_spmd

F32 = mybir.dt.float32
BF16 = mybir.dt.bfloat16
AF = mybir.ActivationFunctionType
ALU = mybir.AluOpType

D = 1024
DC = D // 128
TOK = 2048
SEQ = 4096
TT = 512
NTT = TOK // TT
HID = 4096
EPS = 1e-6
NCORES = 8


class Buf:
    __slots__ = ("name", "w", "r")

    def __init__(self, name=""):
        self.name = name
        self.w = None
        self.r = []


class _Op:
    __slots__ = ("eng", "fn", "deps", "signal", "sem", "val", "dma", "retired", "inc")


_ENGS = ("pe", "act", "dve", "pool", "sp")
_ENG_ATTR = {"pe": "tensor", "act": "scalar", "dve": "vector", "pool": "gpsimd", "sp": "sync"}
_DMA_ENGS = ("sp", "pool")


class Sched:
    def __init__(self, nc, n_dma_sems=8):
        self.nc = nc
        self.ops = {e: [] for e in _ENGS}
        self.n_dma_sems = n_dma_sems
        self.dma_rr = {e: 0 for e in _ENGS}
        self.dma_last = {}
        self.cnt = {e: 0 for e in _ENGS}
        self.waited = {e: {} for e in _ENGS}
        self.sems = {e: nc.alloc_semaphore("s_" + e) for e in ("pe", "act", "dve", "pool")}
        for e in _DMA_ENGS:
            for k in range(n_dma_sems):
                self.sems[(e, k)] = nc.alloc_semaphore("d_%s_%d" % (e, k))
        self.sems[("cc", 0)] = nc.alloc_semaphore("cc_sem")
        self.nops = 0

    def add(self, eng, fn, reads=(), writes=(), dma=False, cc=False):
        op = _Op()
        dma = dma or cc
        op.eng, op.fn, op.dma, op.signal, op.sem, op.val, op.retired = eng, fn, dma, dma, None, 0, False
        op.inc = 1 if cc else 16
        deps, seen = [], set()
        self.nops += 1

        def adddep(d):
            if d is not None and not d.retired and id(d) not in seen:
                seen.add(id(d))
                deps.append(d)

        for b in reads:
            adddep(b.w)
        for b in writes:
            adddep(b.w)
            for r in b.r:
                adddep(r)
        if dma:
            if cc:
                key = ("cc", 0)
            else:
                key = (eng, self.dma_rr[eng] % self.n_dma_sems)
                self.dma_rr[eng] += 1
            prev = self.dma_last.get(key)
            adddep(prev)
            self.dma_last[key] = op
            op.sem = key
            op.val = (prev.val if prev is not None else 0) + op.inc
        fdeps = []
        for d in deps:
            if not d.dma and d.eng == eng and eng == "pe":
                continue
            d.signal = True
            fdeps.append(d)
        op.deps = fdeps
        for b in reads:
            b.r.append(op)
        for b in writes:
            b.w = op
            b.r = []
        self.ops[eng].append(op)
        return op

    def emit(self):
        nc, sems = self.nc, self.sems
        for e in _ENGS:
            c = self.cnt[e]
            for op in self.ops[e]:
                if not op.dma and op.signal:
                    c += 1
                    op.sem, op.val = e, c
            self.cnt[e] = c
        if not any(self.ops[e] for e in _ENGS):
            return
        with nc.Block() as block:
            for e in _ENGS:
                ops = self.ops[e]
                if not ops:
                    continue

                def body(eng, ops=ops, e=e):
                    waited = self.waited[e]
                    last_dma = {}
                    for op in ops:
                        for d in op.deps:
                            if waited.get(d.sem, 0) >= d.val:
                                continue
                            eng.wait_ge(sems[d.sem], d.val)
                            waited[d.sem] = d.val
                        if op.fn is None:
                            continue
                        inst = op.fn(eng)
                        if op.signal:
                            if op.dma and op.inc == 1:
                                inst.then_inc(sems[op.sem])
                            else:
                                inst.then_inc(sems[op.sem], 16 if op.dma else 1)
                        if op.dma:
                            last_dma[op.sem] = op.val
                    for s, v in last_dma.items():
                        if waited.get(s, 0) < v:
                            eng.wait_ge(sems[s], v)
                            waited[s] = v

                getattr(block, _ENG_ATTR[e])(body)
        for e in _ENGS:
            for op in self.ops[e]:
                op.retired = True
                op.fn = None
            self.ops[e] = []


class Phase:
    def __init__(self, cx):
        self.cx = cx
        self.stack = contextlib.ExitStack()
        self.n = 0

    def __enter__(self):
        return self

    def sb(self, name, shape, dtype):
        cx = self.cx
        cx.uid += 1
        return self.stack.enter_context(cx.nc.sbuf_tensor("%s_%d" % (name, cx.uid), list(shape), dtype))

    def __exit__(self, et, ev, tb):
        if et is None:
            self.cx.S.emit()
        self.stack.close()
        return False


class Ctx:
    def __init__(self, nc, with_x=True):
        self.nc = nc
        self.S = Sched(nc)
        self.uid = 0
        S = self.S
        if with_x:
            self.xT = nc.alloc_sbuf_tensor("xT", [128, DC, TOK], F32)
            self.Bx = [Buf("x%d" % t) for t in range(NTT)]
        self.ones = nc.alloc_sbuf_tensor("ones", [128, 128], F32)
        self.ones_bf = nc.alloc_sbuf_tensor("ones_bf", [128, 128], BF16)
        self.Bones = Buf("ones")
        S.add("dve", lambda e: e.memset(self.ones[:], 1.0), writes=[self.Bones])
        S.add("dve", lambda e: e.memset(self.ones_bf[:], 1.0), writes=[self.Bones])
        self.epsb = nc.alloc_sbuf_tensor("epsb", [128, 1], F32)
        self.Beps = Buf("eps")
        S.add("dve", lambda e: e.memset(self.epsb[:], EPS), writes=[self.Beps])
        self.oneb = nc.alloc_sbuf_tensor("oneb", [128, 1], F32)
        S.add("dve", lambda e: e.memset(self.oneb[:], 1.0), writes=[self.Beps])
        self.ps = [nc.alloc_psum_tensor("ps%d" % i, [128, TT], F32) for i in range(8)]
        self.Bps = [Buf("ps%d" % i) for i in range(8)]

    def vec(self, name, dram_vec, nchunks, eng="sp"):
        t = self.nc.alloc_sbuf_tensor(name, [128, nchunks], F32)
        B = Buf(name)
        self.S.add(eng, lambda e: e.dma_start(out=t[:], in_=dram_vec.rearrange("(c p) -> p c", p=128),
                                              allow_slow_non_contiguous=True), writes=[B], dma=True)
        return t, B


def mm(cx, out, lhsT, rhs, start, stop, reads, writes, skip=False):
    cx.S.add("pe", lambda e: e.matmul(out, lhsT=lhsT, rhs=rhs, start=start, stop=stop, skip_group_check=skip),
             reads=reads, writes=writes)


def load_xT(cx, x_dram):
    v = x_dram.rearrange("(c p) n -> p c n", p=128)
    for t in range(NTT):
        sl = slice(t * TT, (t + 1) * TT)
        cx.S.add("sp", lambda e, sl=sl: e.dma_start(out=cx.xT[:, :, sl], in_=v[:, :, sl]),
                 writes=[cx.Bx[t]], dma=True)


def store_T(cx, out_dram, src, Bsrc, ncols=TOK):
    v = out_dram.rearrange("(c p) n -> p c n", p=128)
    Bout = Buf("out")
    for t in range(ncols // TT):
        sl = slice(t * TT, (t + 1) * TT)
        cx.S.add("sp", lambda e, sl=sl: e.dma_start(out=v[:, :, sl], in_=src[:, :, sl]),
                 reads=[Bsrc[t]], writes=[Bout], dma=True)
    cx.S.add("sp", None, reads=[Bout])


def rmsnorm_tile(cx, ph_bufs, t, gT, Bg, dst, Bdst_t, dsl=None):
    S = cx.S
    sq, Bsq, rstd, Brstd = ph_bufs
    sl = slice(t * TT, (t + 1) * TT)
    dsl = sl if dsl is None else dsl
    k = t % 2
    bank = t % 2
    for c in range(DC):
        S.add("act", lambda e, c=c: e.activation(out=sq[k][:], in_=cx.xT[:, c, sl], func=AF.Square),
              reads=[cx.Bx[t]], writes=[Bsq[k]])
        mm(cx, cx.ps[bank][:], cx.ones[:], sq[k][:], c == 0, c == DC - 1, [Bsq[k], cx.Bones], [cx.Bps[bank]])
    S.add("act", lambda e: e.activation(out=rstd[k][:], in_=cx.ps[bank][:], func=AF.Ln, scale=1.0 / D, bias=cx.epsb[:]),
          reads=[cx.Bps[bank], cx.Beps], writes=[Brstd[k]])
    S.add("act", lambda e: e.activation(out=rstd[k][:], in_=rstd[k][:], func=AF.Exp, scale=-0.5),
          reads=[Brstd[k]], writes=[Brstd[k]])
    for c in range(DC):
        S.add("dve", lambda e, c=c: e.scalar_tensor_tensor(
            out=dst[:, c, dsl], in0=cx.xT[:, c, sl], scalar=gT[:, c:c + 1], in1=rstd[k][:],
            op0=ALU.mult, op1=ALU.mult), reads=[cx.Bx[t], Brstd[k], Bg], writes=[Bdst_t])


def norm_bufs(ph):
    sq = [ph.sb("sq%d" % i, [128, TT], F32) for i in range(2)]
    rstd = [ph.sb("rstd%d" % i, [128, TT], F32) for i in range(2)]
    return sq, [Buf("sq0"), Buf("sq1")], rstd, [Buf("rs0"), Buf("rs1")]


HP = 512
NHP = HID // HP
HPC = HP // 128


def mlp_layer(cx, gT, Bg, w1_dram, w2_dram, extra=None):
    S = cx.S
    w1v = w1_dram.rearrange("(c p) h -> p c h", p=128)
    w2v = w2_dram.rearrange("(c p) o -> p c o", p=128)
    with Phase(cx) as ph:
        hT = ph.sb("hT", [128, DC, TOK], BF16)
        Bh = [Buf("h%d" % t) for t in range(NTT)]
        nb = norm_bufs(ph)
        w1 = [ph.sb("w1_%d" % i, [128, DC, HP], BF16) for i in range(2)]
        w2 = [ph.sb("w2_%d" % i, [128, HPC, D], BF16) for i in range(2)]
        Bw1, Bw2 = [Buf(), Buf()], [Buf(), Buf()]
        hid = [ph.sb("hid%d" % i, [128, HPC, TT], BF16) for i in range(2)]
        Bhid = [Buf(), Buf()]
        relu = [ph.sb("relu%d" % i, [128, TT], F32) for i in range(2)]
        Brelu = [Buf(), Buf()]
        def w1_stage(j, t):
            wb = j % 2
            sl = slice(t * TT, (t + 1) * TT)
            hb = t % 2
            if t == 0:
                S.add("pool", lambda e: e.dma_start(out=w1[wb][:], in_=w1v[:, :, j * HP:(j + 1) * HP]),
                      writes=[Bw1[wb]], dma=True)
                S.add("pool", lambda e: e.dma_start(out=w2[wb][:], in_=w2v[:, j * HPC:(j + 1) * HPC, :]),
                      writes=[Bw2[wb]], dma=True)
                if extra:
                    for _ in range(min(3, len(extra))):
                        extra.pop(0)()
            if j == 0:
                rmsnorm_tile(cx, nb, t, gT, Bg, hT, Bh[t])
            for hc in range(HPC):
                bank = 2 + (hc % 2)
                for kc in range(DC):
                    mm(cx, cx.ps[bank][:], w1[wb][:, kc, hc * 128:(hc + 1) * 128], hT[:, kc, sl],
                       kc == 0, kc == DC - 1, [Bw1[wb], Bh[t]], [cx.Bps[bank]])
                rb = hc % 2
                S.add("act", lambda e, bank=bank, rb=rb: e.activation(out=relu[rb][:], in_=cx.ps[bank][:], func=AF.Relu),
                      reads=[cx.Bps[bank]], writes=[Brelu[rb]])
                S.add("dve", lambda e, hc=hc, rb=rb: e.tensor_tensor(
                    out=hid[hb][:, hc, :], in0=relu[rb][:], in1=relu[rb][:], op=ALU.mult),
                    reads=[Brelu[rb]], writes=[Bhid[hb]])

        def w2_stage(j, t):
            wb = j % 2
            sl = slice(t * TT, (t + 1) * TT)
            hb = t % 2
            for oc in range(DC):
                bank = 4 + (oc % 3)
                for hc in range(HPC):
                    mm(cx, cx.ps[bank][:], w2[wb][:, hc, oc * 128:(oc + 1) * 128], hid[hb][:, hc, :],
                       hc == 0, hc == HPC - 1, [Bw2[wb], Bhid[hb]], [cx.Bps[bank]])
                S.add("dve", lambda e, oc=oc, bank=bank: e.tensor_tensor(
                    out=cx.xT[:, oc, sl], in0=cx.xT[:, oc, sl], in1=cx.ps[bank][:], op=ALU.add),
                    reads=[cx.Bps[bank], cx.Bx[t]], writes=[cx.Bx[t]])

        steps = [(j, t) for j in range(NHP) for t in range(NTT)]
        w1_stage(*steps[0])
        for si, st in enumerate(steps):
            if si + 1 < len(steps):
                w1_stage(*steps[si + 1])
            w2_stage(*st)


def p1_phase(cx, gT, Bg, hT_dram, dst_fn=None):
    with Phase(cx) as ph:
        hT = ph.sb("hT", [128, DC, TOK], BF16)
        Bh = [Buf("h%d" % t) for t in range(NTT)]
        nb = norm_bufs(ph)
        for t in range(NTT):
            rmsnorm_tile(cx, nb, t, gT, Bg, hT, Bh[t])
        if dst_fn is None:
            store_T(cx, hT_dram, hT, Bh)
        else:
            Bout = Buf()
            for t in range(NTT):
                sl = slice(t * TT, (t + 1) * TT)
                dst = dst_fn(t)
                cx.S.add("sp", lambda e, sl=sl, dst=dst: e.dma_start(out=dst, in_=hT[:, :, sl]), reads=[Bh[t]], writes=[Bout], dma=True)
            cx.S.add("sp", None, reads=[Bout])


def final_phase(cx, gT, Bg, out_dram):
    with Phase(cx) as ph:
        nb = norm_bufs(ph)
        for t in range(NTT):
            rmsnorm_tile(cx, nb, t, gT, Bg, cx.xT, cx.Bx[t])
        store_T(cx, out_dram, cx.xT, cx.Bx)


def p3_phase(cx, oT_dram, wout_dram):
    S = cx.S
    with Phase(cx) as ph:
        oT = ph.sb("oT", [128, DC, TOK], BF16)
        Bo = [Buf() for _ in range(NTT)]
        w = ph.sb("wout", [128, DC, D], BF16)
        Bw = Buf()
        ov = oT_dram.rearrange("(c p) n -> p c n", p=128)
        S.add("pool", lambda e: e.dma_start(out=w[:], in_=wout_dram.rearrange("(c p) o -> p c o", p=128)),
              writes=[Bw], dma=True)
        for t in range(NTT):
            sl = slice(t * TT, (t + 1) * TT)
            S.add("sp", lambda e, sl=sl: e.dma_start(out=oT[:, :, sl], in_=ov[:, :, sl]), writes=[Bo[t]], dma=True)
        for t in range(NTT):
            sl = slice(t * TT, (t + 1) * TT)
            for oc in range(DC):
                bank = oc % 4
                for kc in range(DC):
                    mm(cx, cx.ps[bank][:], w[:, kc, oc * 128:(oc + 1) * 128], oT[:, kc, sl], kc == 0, kc == DC - 1,
                       [Bw, Bo[t]], [cx.Bps[bank]])
                S.add("dve", lambda e, oc=oc, sl=sl, bank=bank: e.tensor_tensor(
                    out=cx.xT[:, oc, sl], in0=cx.xT[:, oc, sl], in1=cx.ps[bank][:], op=ALU.add),
                    reads=[cx.Bps[bank], cx.Bx[t]], writes=[cx.Bx[t]])


GF = 3072
GFC = GF // 128
GG = 8
GPC = 3


class _Scratch:
    pass


def gmlp_scratch(cx):
    sc = _Scratch()
    cx.uid += 1
    sc.win_c = cx.nc.dram_tensor("gwin_bf_%d" % cx.uid, [12, 128, DC * 512], BF16).ap()
    sc.wout_c = cx.nc.dram_tensor("gwout_bf_%d" % cx.uid, [6, 128, 4 * D], BF16).ap()
    sc.Bwin_c = [Buf() for _ in range(12)]
    sc.Bwout_c = [Buf() for _ in range(6)]
    return sc


def gmlp_precast_ops(cx, sc, win_dram, wout_dram):
    S = cx.S
    winv = win_dram.rearrange("(c p) f -> p c f", p=128)
    woutv = wout_dram.rearrange("(c p) o -> p c o", p=128)
    ops = []
    for p_ in range(12):
        ops.append(lambda p_=p_: S.add("pool", lambda e: e.dma_start(out=sc.win_c[p_].rearrange("q (c f) -> q c f", c=DC),
                                                                    in_=winv[:, :, p_ * 512:(p_ + 1) * 512]),
                                       writes=[sc.Bwin_c[p_]], dma=True))
    for q in range(6):
        ops.append(lambda q=q: S.add("pool", lambda e: e.dma_start(out=sc.wout_c[q].rearrange("q (c f) -> q c f", c=4),
                                                                  in_=woutv[:, q * 4:(q + 1) * 4, :]),
                                     writes=[sc.Bwout_c[q]], dma=True))
    return ops


def gmlp_phase(cx, gT, Bg, win_dram, lng, Blng, lnb, Blnb, wsT_dram, bs_dram, wout_dram, pre=None):
    S = cx.S
    winv = win_dram.rearrange("(c p) f -> p c f", p=128)
    woutv = wout_dram.rearrange("(c p) o -> p c o", p=128)
    with Phase(cx) as ph:
        wgf = ph.sb("wgf", [128, GG * 128], F32)
        Bwgf = Buf()
        S.add("sp", lambda e: e.dma_start(out=wgf[:].rearrange("s (g t) -> s g t", g=GG),
                                          in_=wsT_dram.rearrange("g s t -> s g t")), writes=[Bwgf], dma=True)
        S.add("pool", lambda e: e.affine_select(out=wgf[:].rearrange("s (g t) -> s g t", g=GG),
                                                in_=wgf[:].rearrange("s (g t) -> s g t", g=GG),
                                                pattern=[[0, GG], [1, 128]], compare_op=ALU.is_ge, fill=0.0,
                                                base=0, channel_multiplier=-1), reads=[Bwgf], writes=[Bwgf])
        wg = ph.sb("wg", [128, GG * 128], BF16)
        Bwg = Buf()
        S.add("dve", lambda e: e.tensor_copy(out=wg[:], in_=wgf[:]), reads=[Bwgf], writes=[Bwg])
        bsrow = ph.sb("bsrow", [1, GG * 128], F32)
        Bbsrow = Buf()
        S.add("sp", lambda e: e.dma_start(out=bsrow[:], in_=bs_dram.rearrange("(o g) t -> o (g t)", o=1)),
              writes=[Bbsrow], dma=True)
        bsb = ph.sb("bsb", [128, GG * 128], F32)
        Bbsb = Buf()
        C2 = ph.sb("C2", [128, GFC * 128], F32)
        BC2 = Buf()
        for half in range(2):
            cs = slice(half * 512, (half + 1) * 512)
            mm(cx, cx.ps[5][:], cx.ones[0:1, :], bsrow[0:1, cs], True, True, [Bbsrow, cx.Bones], [cx.Bps[5]])
            S.add("act", lambda e, cs=cs: e.activation(out=bsb[:, cs], in_=cx.ps[5][:], func=AF.Copy),
                  reads=[cx.Bps[5]], writes=[Bbsb])
            mm(cx, cx.ps[6][:], cx.ones[:], wgf[:, cs], True, True, [Bwgf, cx.Bones], [cx.Bps[6]])
            for gl in range(4):
                g = half * 4 + gl
                for j in range(GPC):
                    fc = g * GPC + j
                    S.add("dve", lambda e, gl=gl, g=g, fc=fc: e.scalar_tensor_tensor(
                        out=C2[:, fc * 128:(fc + 1) * 128], in0=cx.ps[6][:, gl * 128:(gl + 1) * 128],
                        scalar=lnb[:, fc:fc + 1], in1=bsb[:, g * 128:(g + 1) * 128], op0=ALU.mult, op1=ALU.add),
                        reads=[cx.Bps[6], Bbsb, Blnb], writes=[BC2])
        hTg = ph.sb("hTg", [128, DC, TT], BF16)
        Bhg = Buf()
        nb = norm_bufs(ph)
        NWIN = 3
        win = [ph.sb("win%d" % i, [128, DC, 512], BF16) for i in range(NWIN)]
        Bwin = [Buf() for _ in range(NWIN)]
        wout = [ph.sb("wo%d" % i, [128, 4, D], BF16) for i in range(2)]
        Bwout = [Buf(), Buf()]
        uT = ph.sb("uT", [128, GFC, TT], BF16)
        Bu = [Buf() for _ in range(GFC)]
        vt = ph.sb("vt", [128, 4, GF], BF16)
        Bv = [Buf() for _ in range(4)]
        tmpf = [ph.sb("tmpf%d" % i, [128, 512], F32) for i in range(2)]
        Btmpf = [Buf(), Buf()]
        stats = ph.sb("stats", [128, 4 * 6 * 6], F32)
        Bst = [Buf() for _ in range(4)]
        mv = ph.sb("mv", [128, 8], F32)
        Bmv = Buf()
        lrs = ph.sb("lrs", [128, 4], F32)
        Blrs = Buf()
        tmpm = [wgf[:, i * 512:(i + 1) * 512] for i in range(2)]
        Btmpm = [Bwgf, Bwgf]
        wstep = 0
        ostep = 0
        if pre is None:
            pre_done = False
            sc = gmlp_scratch(cx)
        else:
            pre_done = True
            sc = pre
        win_c, wout_c, Bwin_c, Bwout_c = sc.win_c, sc.wout_c, sc.Bwin_c, sc.Bwout_c
        for tg in range(NTT):
            sl = slice(tg * TT, (tg + 1) * TT)
            rmsnorm_tile(cx, nb, tg, gT, Bg, hTg, Bhg, dsl=slice(0, TT))
            nv = 0
            for p in range(12):
                wb = wstep % NWIN
                wstep += 1
                if tg == 0 and not pre_done:
                    S.add("pool", lambda e, p=p, wb=wb: e.dma_start(out=win[wb][:], in_=winv[:, :, p * 512:(p + 1) * 512]),
                          writes=[Bwin[wb]], dma=True)
                    S.add("sp", lambda e, p=p, wb=wb: e.dma_start(out=win_c[p].rearrange("q (c f) -> q c f", c=DC), in_=win[wb][:]),
                          reads=[Bwin[wb]], writes=[Bwin_c[p]], dma=True)
                else:
                    S.add("sp", lambda e, p=p, wb=wb: e.dma_start(out=win[wb][:], in_=win_c[p].rearrange("q (c f) -> q c f", c=DC)),
                          reads=[Bwin_c[p]], writes=[Bwin[wb]], dma=True)
                if p < 6:
                    for j in range(4):
                        fc = p * 4 + j
                        bank = 2 + (j % 2)
                        for kc in range(DC):
                            mm(cx, cx.ps[bank][:], win[wb][:, kc, j * 128:(j + 1) * 128], hTg[:, kc, :],
                               kc == 0, kc == DC - 1, [Bwin[wb], Bhg], [cx.Bps[bank]])
                        S.add("act", lambda e, fc=fc, bank=bank: e.activation(out=uT[:, fc, :], in_=cx.ps[bank][:], func=AF.Gelu),
                              reads=[cx.Bps[bank]], writes=[Bu[fc]])
                else:
                    pv = p - 6
                    for ch in range(4):
                        bank = 2 + (ch % 2)
                        for kc in range(DC):
                            mm(cx, cx.ps[bank][:], hTg[:, kc, ch * 128:(ch + 1) * 128], win[wb][:, kc, :],
                               kc == 0, kc == DC - 1, [Bwin[wb], Bhg], [cx.Bps[bank]])
                        k = nv % 2
                        nv += 1
                        S.add("act", lambda e, k=k, bank=bank: e.activation(out=tmpf[k][:], in_=cx.ps[bank][:], func=AF.Gelu),
                              reads=[cx.Bps[bank]], writes=[Btmpf[k]])
                        S.add("dve", lambda e, k=k, ch=ch, pv=pv: e.bn_stats(
                            out=stats[:, (ch * 6 + pv) * 6:(ch * 6 + pv + 1) * 6], in_=tmpf[k][:]),
                            reads=[Btmpf[k]], writes=[Bst[ch]])
                        S.add("pool", lambda e, k=k, ch=ch, pv=pv: e.tensor_copy(
                            out=vt[:, ch, pv * 512:(pv + 1) * 512], in_=tmpf[k][:]),
                            reads=[Btmpf[k]], writes=[Bv[ch]])
            for ch in range(4):
                S.add("dve", lambda e, ch=ch: e.bn_aggr(out=mv[:, 2 * ch:2 * ch + 2], in_=stats[:, ch * 36:(ch + 1) * 36]),
                      reads=[Bst[ch]], writes=[Bmv])
            mv3 = mv[:].rearrange("p (c two) -> p c two", two=2)
            S.add("act", lambda e: e.activation(out=lrs[:], in_=mv3[:, :, 1], func=AF.Ln, bias=cx.epsb[:]),
                  reads=[Bmv, cx.Beps], writes=[Blrs])
            S.add("act", lambda e: e.activation(out=lrs[:], in_=lrs[:], func=AF.Exp, scale=-0.5),
                  reads=[Blrs], writes=[Blrs])
            for ch in range(4):
                S.add("dve", lambda e, ch=ch: e.tensor_scalar(
                    out=vt[:, ch, :], in0=vt[:, ch, :], scalar1=mv[:, 2 * ch:2 * ch + 1], scalar2=lrs[:, ch:ch + 1],
                    op0=ALU.subtract, op1=ALU.mult), reads=[Bv[ch], Bmv, Blrs], writes=[Bv[ch]])
            def gate_piece(q):
                for fc in range(4 * q, 4 * q + 4):
                    g = fc // GPC
                    bank = fc % 2
                    k = fc % 2
                    for ch in range(4):
                        mm(cx, cx.ps[bank][:, ch * 128:(ch + 1) * 128], vt[:, ch, fc * 128:(fc + 1) * 128],
                           wg[:, g * 128:(g + 1) * 128], True, True, [Bv[ch], Bwg], [cx.Bps[bank]])
                    S.add("dve", lambda e, fc=fc, bank=bank, k=k: e.scalar_tensor_tensor(
                        out=tmpm[k].rearrange("p (c t) -> p c t", c=4), in0=cx.ps[bank][:].rearrange("p (c t) -> p c t", c=4),
                        scalar=lng[:, fc:fc + 1],
                        in1=C2[:, fc * 128:(fc + 1) * 128].unsqueeze(1).to_broadcast([128, 4, 128]), op0=ALU.mult, op1=ALU.add),
                        reads=[cx.Bps[bank], Blng, BC2], writes=[Btmpm[k]])
                    S.add("dve", lambda e, fc=fc, k=k: e.tensor_tensor(out=uT[:, fc, :], in0=tmpm[k], in1=uT[:, fc, :], op=ALU.mult),
                          reads=[Btmpm[k], Bu[fc]], writes=[Bu[fc]])

            def out_piece(q, tg=tg, sl=sl):
                nonlocal ostep
                wb = ostep % 2
                ostep += 1
                if tg == 0 and not pre_done:
                    S.add("pool", lambda e: e.dma_start(out=wout[wb][:], in_=woutv[:, q * 4:(q + 1) * 4, :]),
                          writes=[Bwout[wb]], dma=True)
                    S.add("sp", lambda e: e.dma_start(out=wout_c[q].rearrange("q (c f) -> q c f", c=4), in_=wout[wb][:]),
                          reads=[Bwout[wb]], writes=[Bwout_c[q]], dma=True)
                else:
                    S.add("sp", lambda e: e.dma_start(out=wout[wb][:], in_=wout_c[q].rearrange("q (c f) -> q c f", c=4)),
                          reads=[Bwout_c[q]], writes=[Bwout[wb]], dma=True)
                for oc in range(DC):
                    bank = 4 + (oc % 3)
                    for j in range(4):
                        mm(cx, cx.ps[bank][:], wout[wb][:, j, oc * 128:(oc + 1) * 128], uT[:, q * 4 + j, :],
                           j == 0, j == 3, [Bwout[wb], Bu[q * 4 + j]], [cx.Bps[bank]])
                    S.add("dve", lambda e, oc=oc, bank=bank: e.tensor_tensor(
                        out=cx.xT[:, oc, sl], in0=cx.xT[:, oc, sl], in1=cx.ps[bank][:], op=ALU.add),
                        reads=[cx.Bps[bank], cx.Bx[tg]], writes=[cx.Bx[tg]])

            gate_piece(0)
            for q in range(6):
                if q + 1 < 6:
                    gate_piece(q + 1)
                out_piece(q)


def _dram_in(nc, name, shape, dt=F32):
    return nc.dram_tensor(name, list(shape), dt, kind="ExternalInput").ap()


def _dram_out(nc, name, shape, dt=F32):
    return nc.dram_tensor(name, list(shape), dt, kind="ExternalOutput").ap()


def odd_decl(nc, tag):
    d = _Scratch()
    d.tag = tag
    d.mixg = _dram_in(nc, tag + "mixg", [D])
    d.win = _dram_in(nc, tag + "gwin", [D, 2 * GF])
    d.lng = _dram_in(nc, tag + "lng", [GF])
    d.lnb = _dram_in(nc, tag + "lnb", [GF])
    d.wsT = _dram_in(nc, tag + "wsT", [GG, 128, 128])
    d.bs = _dram_in(nc, tag + "bs", [GG, 128])
    d.wout = _dram_in(nc, tag + "gwout", [GF, D])
    d.mlpg = _dram_in(nc, tag + "mlpg", [D])
    d.w1 = _dram_in(nc, tag + "w1", [D, HID])
    d.w2 = _dram_in(nc, tag + "w2", [HID, D])
    return d


def odd_run(cx, d, pre=None):
    tag = d.tag
    gT, Bg = cx.vec(tag + "mixg_sb", d.mixg, DC)
    lngT, Blng = cx.vec(tag + "lng_sb", d.lng, GFC)
    lnbT, Blnb = cx.vec(tag + "lnb_sb", d.lnb, GFC)
    mgT, Bmg = cx.vec(tag + "mlpg_sb", d.mlpg, DC)
    gmlp_phase(cx, gT, Bg, d.win, lngT, Blng, lnbT, Blnb, d.wsT, d.bs, d.wout, pre=pre)
    mlp_layer(cx, mgT, Bmg, d.w1, d.w2)


def odd_layer(cx, nc, tag):
    odd_run(cx, odd_decl(nc, tag))


def build_test_odd():
    nc = bass.Bass("TRN2", target_bir_lowering=False)
    xin = _dram_in(nc, "xT_in", [D, TOK])
    xout = _dram_out(nc, "xT_out", [D, TOK])
    cx = Ctx(nc)
    load_xT(cx, xin)
    odd_layer(cx, nc, "o_")
    with Phase(cx):
        store_T(cx, xout, cx.xT, cx.Bx)
    return nc


WSEL = 7 * 2 * 128
NT8 = SEQ // TT
NBLK = SEQ // 128
NCH = SEQ // 64
MID = 31
SB_SCALE = 128 ** -0.5


def _col(kind, h):
    return (kind if kind < 4 else kind - 4) * 128


def _head_cols(is_b, h):
    return (1024 + h * 384, 384) if is_b else (h * 512, 512)


class P2Ctx:
    pass


def p2_consts(cx, nc):
    S = cx.S
    p = P2Ctx()
    p.hstep = 0
    p.lb = nc.alloc_sbuf_tensor("lb", [128, 2], F32)
    p.oml = nc.alloc_sbuf_tensor("oml", [128, 2], F32)
    p.Blb = Buf()

    def head_bufs(ph, col0, ncols, new_ring=True):
        w = ph.sb("wh", [128, DC, ncols], BF16)
        Bw = Buf()
        S.add("pool", lambda e: e.dma_start(out=w[:], in_=p.wv[:, :, col0:col0 + ncols]), writes=[Bw], dma=True)
        p.w, p.Bw = w, Bw
        if new_ring:
            p.hring = [ph.sb("hring%d" % i, [128, DC, TT], BF16) for i in range(2)]
            p.Bhring = [Buf(), Buf()]
    p.head_bufs = head_bufs

    def load_h(t):
        k = p.hstep % 2
        p.hstep += 1
        src = p.h_tile(t)
        ring, B = p.hring[k], p.Bhring[k]
        S.add("sp", lambda e: e.dma_start(out=ring[:], in_=src), reads=[p.Bhsrc], writes=[B], dma=True)
        return ring, B
    p.load_h = load_h
    def mask(name, dt, pattern, cm, op, base_val, fill):
        t = nc.alloc_sbuf_tensor(name, [128, 128], dt)
        B = Buf()
        S.add("pool", lambda e: e.memset(t[:], base_val), writes=[B])
        S.add("pool", lambda e: e.affine_select(out=t[:], in_=t[:], pattern=pattern, compare_op=op, fill=fill,
                                                base=0, channel_multiplier=cm), reads=[B], writes=[B])
        return t, B
    p.ident, p.Bident = mask("ident", BF16, [[1, 128]], -1, ALU.is_equal, 1.0, 0.0)
    p.bdm, p.Bbdm = mask("bdm", F32, [[1, 128]], -1, ALU.is_ge, 1.0, 0.0)
    S.add("pool", lambda e: e.memset(p.bdm[0:64, 64:128], 0.0), reads=[p.Bbdm], writes=[p.Bbdm])
    p.U, p.BU = mask("Umask", BF16, [[-1, 128]], 1, ALU.is_gt, 1.0, 0.0)
    p.V, p.BV = mask("Vmask", BF16, [[1, 128]], -1, ALU.is_ge, 1.0, 0.0)
    p.Lm, p.BLm = mask("Lmask", F32, [[1, 128]], -1, ALU.is_gt, 1.0, 0.0)
    p.mb, p.Bmb = mask("mbias", F32, [[1, 128]], -1, ALU.is_gt, 0.0, -30000.0)
    p.lom = nc.alloc_sbuf_tensor("lom", [128, 2], F32)
    p.Blom = Buf()
    S.add("pool", lambda e: e.memset(p.lom[:], 1.0), writes=[p.Blom])
    S.add("pool", lambda e: e.affine_select(out=p.lom[:, 0:1], in_=p.lom[:, 0:1], pattern=[[0, 1]], compare_op=ALU.is_ge,
                                            fill=0.0, base=63, channel_multiplier=-1), reads=[p.Blom], writes=[p.Blom])
    S.add("pool", lambda e: e.affine_select(out=p.lom[:, 1:2], in_=p.lom[:, 1:2], pattern=[[0, 1]], compare_op=ALU.is_ge,
                                            fill=0.0, base=-64, channel_multiplier=1), reads=[p.Blom], writes=[p.Blom])
    p.rmask = nc.alloc_sbuf_tensor("rmask", [128, TT], F32)
    p.Brmask = Buf()
    S.add("pool", lambda e: e.memset(p.rmask[:], 1.0), writes=[p.Brmask])
    S.add("pool", lambda e: e.memset(p.rmask[:].rearrange("p (c s) -> p c s", s=64)[:, :, 0:1], 0.0),
          reads=[p.Brmask], writes=[p.Brmask])
    return p


def p2_layer(cx, p, e_idx, tag, h_tile, Bhsrc, w_d, lbl_d, gn_d):
    S = cx.S
    p.h_tile, p.Bhsrc = h_tile, Bhsrc
    p.wv = w_d.rearrange("(c p) f -> p c f", p=128)
    p.gn, p.Bgn = cx.vec(tag + "gn_sb", gn_d, 2)
    if e_idx == 0:
        S.add("dve", lambda e: e.memset(p.lb[:], 0.0), writes=[p.Blb])
        S.add("dve", lambda e: e.memset(p.oml[:], 1.0), writes=[p.Blb])
    else:
        l0, Bl0 = cx.vec(tag + "l0_sb", lbl_d[0], 2)
        l1, Bl1 = cx.vec(tag + "l1_sb", lbl_d[1], 2)
        S.add("dve", lambda e: e.tensor_tensor(out=p.lb[:], in0=l1[:], in1=l0[:], op=ALU.subtract),
              reads=[Bl0, Bl1], writes=[p.Blb])
        S.add("act", lambda e: e.activation(out=p.lb[:], in_=p.lb[:], func=AF.Sigmoid), reads=[p.Blb], writes=[p.Blb])
        S.add("dve", lambda e: e.tensor_scalar(out=p.oml[:], in0=p.lb[:], scalar1=-1.0, scalar2=1.0,
                                               op0=ALU.mult, op1=ALU.add), reads=[p.Blb], writes=[p.Blb])


def _proj_fm(cx, p, bank, col0, ht, Bht):
    for kc in range(DC):
        mm(cx, cx.ps[bank][:], p.w[:, kc, col0:col0 + 128], ht[:, kc, :], kc == 0, kc == DC - 1,
           [p.Bw, Bht], [cx.Bps[bank]])


def _proj_tm(cx, p, bank, col0, ht, Bht):
    for b4 in range(4):
        for kc in range(DC):
            mm(cx, cx.ps[bank][:, b4 * 128:(b4 + 1) * 128], ht[:, kc, b4 * 128:(b4 + 1) * 128], p.w[:, kc, col0:col0 + 128],
               kc == 0, kc == DC - 1, [p.Bw, Bht], [cx.Bps[bank]])


def hgrn_head(cx, p, h, oT_d):
    S = cx.S
    with Phase(cx) as ph:
        qt = ph.sb("qt", [128, SEQ], BF16)
        kt = ph.sb("kt", [128, SEQ], BF16)
        Bqt = [Buf() for _ in range(NT8)]
        Bkt = [Buf() for _ in range(NT8)]
        ktlo = [ph.sb("ktlo%d" % i, [128, 128], BF16) for i in range(2)]
        kthi = [ph.sb("kthi%d" % i, [128, 128], BF16) for i in range(2)]
        Bktlo, Bkthi = [Buf(), Buf()], [Buf(), Buf()]
        vtok = ph.sb("vtok", [128, SEQ], BF16)
        Bvtok = [Buf() for _ in range(NT8)]
        gate = ph.sb("gate", [128, SEQ], BF16)
        Bgate = [Buf() for _ in range(NT8)]
        bl = ph.sb("bl", [128, NCH], F32)
        bm = ph.sb("bm", [128, NCH], F32)
        Bblm = Buf()
        c1 = ph.sb("c1", [128, NCH], F32)
        c2 = ph.sb("c2", [128, NCH], F32)
        cm = ph.sb("cm", [128, NCH], F32)
        Bc = Buf()
        def dbl(name, dt=F32, w=TT):
            return [ph.sb("%s%d" % (name, i), [128, w], dt) for i in range(2)], [Buf(), Buf()]
        qf, Bqf = dbl("qf")
        sg, Bsg = dbl("sg")
        lf, Blf = dbl("lf")
        kk, Bkk = dbl("kk")
        bb, Bbb = dbl("bb")
        bp, Bbp = dbl("bp")
        Ep, BEp = dbl("Ep")
        Em, BEm = dbl("Em")
        gf, Bgf = dbl("gf")
        p.head_bufs(ph, *_head_cols(False, h))
        for t in range(NT8):
            k = t % 2
            sl = slice(t * TT, (t + 1) * TT)
            ht, Bht = p.load_h(t)
            _proj_fm(cx, p, 0, _col(0, h), ht, Bht)
            S.add("act", lambda e, k=k: e.activation(out=qf[k][:], in_=cx.ps[0][:], func=AF.Silu),
                  reads=[cx.Bps[0]], writes=[Bqf[k]])
            _proj_fm(cx, p, 1, _col(1, h), ht, Bht)
            S.add("act", lambda e, k=k: e.activation(out=sg[k][:], in_=cx.ps[1][:], func=AF.Sigmoid),
                  reads=[cx.Bps[1]], writes=[Bsg[k]])
            S.add("dve", lambda e, k=k: e.tensor_scalar(out=sg[k][:], in0=sg[k][:], scalar1=p.oml[:, h:h + 1],
                                                        scalar2=p.lb[:, h:h + 1], op0=ALU.mult, op1=ALU.add),
                  reads=[Bsg[k], p.Blb], writes=[Bsg[k]])
            S.add("act", lambda e, k=k: e.activation(out=lf[k][:], in_=sg[k][:], func=AF.Ln),
                  reads=[Bsg[k]], writes=[Blf[k]])
            S.add("dve", lambda e, k=k: e.tensor_scalar(out=kk[k][:], in0=sg[k][:], scalar1=-1.0, scalar2=1.0,
                                                        op0=ALU.mult, op1=ALU.add), reads=[Bsg[k]], writes=[Bkk[k]])
            S.add("dve", lambda e, k=k: e.tensor_tensor_scan(out=bb[k][:], data0=p.rmask[:], data1=lf[k][:], initial=0.0,
                                                             op0=ALU.mult, op1=ALU.add),
                  reads=[Blf[k], p.Brmask], writes=[Bbb[k]])
            b3 = lambda k: bb[k][:].rearrange("p (c s) -> p c s", s=64)
            S.add("dve", lambda e, k=k: e.tensor_tensor(out=bp[k][:].rearrange("p (c s) -> p c s", s=64), in0=b3(k),
                                                        in1=b3(k)[:, :, MID:MID + 1].to_broadcast([128, 8, 64]),
                                                        op=ALU.subtract), reads=[Bbb[k]], writes=[Bbp[k]])
            S.add("dve", lambda e, k=k, t=t: e.tensor_copy(out=bl[:, t * 8:(t + 1) * 8], in_=b3(k)[:, :, 63]),
                  reads=[Bbb[k]], writes=[Bblm])
            S.add("dve", lambda e, k=k, t=t: e.tensor_copy(out=bm[:, t * 8:(t + 1) * 8], in_=b3(k)[:, :, MID]),
                  reads=[Bbb[k]], writes=[Bblm])
            S.add("act", lambda e, k=k: e.activation(out=Ep[k][:], in_=bp[k][:], func=AF.Exp), reads=[Bbp[k]], writes=[BEp[k]])
            S.add("act", lambda e, k=k: e.activation(out=Em[k][:], in_=bp[k][:], func=AF.Exp, scale=-1.0),
                  reads=[Bbp[k]], writes=[BEm[k]])
            S.add("dve", lambda e, k=k, sl=sl: e.tensor_tensor(out=qt[:, sl], in0=qf[k][:], in1=Ep[k][:], op=ALU.mult),
                  reads=[Bqf[k], BEp[k]], writes=[Bqt[t]])
            S.add("dve", lambda e, k=k, sl=sl: e.tensor_tensor(out=kt[:, sl], in0=kk[k][:], in1=Em[k][:], op=ALU.mult),
                  reads=[Bkk[k], BEm[k]], writes=[Bkt[t]])
            _proj_fm(cx, p, 2, _col(3, h), ht, Bht)
            S.add("act", lambda e, k=k: e.activation(out=gf[k][:], in_=cx.ps[2][:], func=AF.Silu),
                  reads=[cx.Bps[2]], writes=[Bgf[k]])
            S.add("dve", lambda e, k=k, sl=sl: e.tensor_scalar(out=gate[:, sl], in0=gf[k][:], scalar1=p.gn[:, h:h + 1],
                                                               scalar2=None, op0=ALU.mult),
                  reads=[Bgf[k], p.Bgn], writes=[Bgate[t]])
            _proj_tm(cx, p, 3, _col(2, h), ht, Bht)
            S.add("act", lambda e, sl=sl: e.activation(out=vtok[:, sl], in_=cx.ps[3][:], func=AF.Copy),
                  reads=[cx.Bps[3]], writes=[Bvtok[t]])
        S.add("act", lambda e: e.activation(out=c1[:], in_=bl[:], func=AF.Exp), reads=[Bblm], writes=[Bc])
        S.add("act", lambda e: e.activation(out=cm[:], in_=bm[:], func=AF.Exp), reads=[Bblm], writes=[Bc])
        S.add("dve", lambda e: e.tensor_tensor(out=c2[:], in0=bl[:], in1=bm[:], op=ALU.subtract), reads=[Bblm], writes=[Bc])
        S.add("act", lambda e: e.activation(out=c2[:], in_=c2[:], func=AF.Exp), reads=[Bc], writes=[Bc])
        if "nostep3" in P2_PARTS:
            return
        Sst = ph.sb("Sst", [128, 128], F32)
        BSst = Buf()
        St = [ph.sb("St%d" % i, [128, 128], BF16) for i in range(2)]
        BSt = [Buf(), Buf()]
        Xs = [ph.sb("Xs%d" % i, [128, 128], F32) for i in range(2)]
        BXs = [Buf(), Buf()]
        S.add("dve", lambda e: e.memset(Sst[:], 0.0), writes=[BSst])
        S.add("dve", lambda e: e.memset(St[0][:], 0.0), writes=[BSt[0]])
        scm, Bscm = dbl("scm", BF16, 128)
        sqf, Bsqf = dbl("sqf")
        rr, Brr = dbl("rr")
        o1, Bo1 = dbl("o1")
        ost, Bost = dbl("ost", BF16)
        Bout = Buf()
        def stage_a(i):
            t = i // 4
            k = i % 2
            cs = slice(i * 128, (i + 1) * 128)
            mm(cx, cx.ps[6][:, k * 128:(k + 1) * 128], kt[:, cs], p.ident[:], True, True, [Bkt[t], p.Bident], [cx.Bps[6]])
            S.add("act", lambda e, k=k: e.activation(out=ktlo[k][:], in_=cx.ps[6][:, k * 128:(k + 1) * 128], func=AF.Copy,
                                                     scale=p.lom[:, 0:1]), reads=[cx.Bps[6], p.Blom], writes=[Bktlo[k]])
            S.add("act", lambda e, k=k: e.activation(out=kthi[k][:], in_=cx.ps[6][:, k * 128:(k + 1) * 128], func=AF.Copy,
                                                     scale=p.lom[:, 1:2]), reads=[cx.Bps[6], p.Blom], writes=[Bkthi[k]])
            sb_ = i % 2
            mm(cx, cx.ps[sb_][:, 0:128], kt[:, cs], qt[:, cs], True, True, [Bkt[t], Bqt[t]], [cx.Bps[sb_]])
            S.add("dve", lambda e, k=k, sb_=sb_: e.tensor_tensor(out=scm[k][:], in0=cx.ps[sb_][:, 0:128], in1=p.bdm[:], op=ALU.mult),
                  reads=[cx.Bps[sb_], p.Bbdm], writes=[Bscm[k]])
            xb = 2 + i % 2
            mm(cx, cx.ps[xb][:, 0:128], ktlo[k][:], vtok[:, cs], True, True, [Bktlo[k], Bvtok[t]], [cx.Bps[xb]])
            mm(cx, cx.ps[xb][:, 128:256], kthi[k][:], vtok[:, cs], True, True, [Bkthi[k], Bvtok[t]], [cx.Bps[xb]])

        def stage_b(i):
            t = i // 4
            k = i % 2
            cs = slice(i * 128, (i + 1) * 128)
            xb = 2 + i % 2
            ob = 4 + (i // 4) % 2
            oc = (i % 4) * 128
            mm(cx, cx.ps[ob][:, oc:oc + 128], vtok[:, cs], scm[k][:], True, False, [Bvtok[t], Bscm[k]], [cx.Bps[ob]])
            for half in range(2):
                c = 2 * i + half
                sp_ = c % 2
                qs = slice(i * 128 + half * 64, i * 128 + (half + 1) * 64)
                mm(cx, cx.ps[ob][:, oc + half * 64:oc + (half + 1) * 64], St[sp_][:], qt[:, qs], False, half == 1,
                   [BSt[sp_], Bqt[t]], [cx.Bps[ob]])
                xk = c % 2
                S.add("act", lambda e, c=c, xb=xb, half=half, xk=xk: e.activation(
                    out=Xs[xk][:], in_=cx.ps[xb][:, half * 128:(half + 1) * 128], func=AF.Copy, scale=c2[:, c:c + 1]),
                    reads=[cx.Bps[xb], Bc], writes=[BXs[xk]])
                S.add("dve", lambda e, c=c, xk=xk: e.scalar_tensor_tensor(
                    out=Sst[:], in0=Sst[:], scalar=c1[:, c:c + 1], in1=Xs[xk][:], op0=ALU.mult, op1=ALU.add),
                    reads=[BSst, BXs[xk], Bc], writes=[BSst])
                if c + 1 < NCH:
                    S.add("dve", lambda e, c=c, sp_=sp_: e.tensor_scalar(out=St[1 - sp_][:], in0=Sst[:], scalar1=cm[:, c + 1:c + 2],
                                                                          scalar2=None, op0=ALU.mult),
                          reads=[BSst, Bc], writes=[BSt[1 - sp_]])
            if i % 4 == 3:
                k2 = t % 2
                sl = slice(t * TT, (t + 1) * TT)
                S.add("act", lambda e, k2=k2, ob=ob: e.activation(out=sqf[k2][:], in_=cx.ps[ob][:], func=AF.Square),
                      reads=[cx.Bps[ob]], writes=[Bsqf[k2]])
                mm(cx, cx.ps[6][:], cx.ones[:], sqf[k2][:], True, True, [Bsqf[k2], cx.Bones], [cx.Bps[6]])
                S.add("act", lambda e, k2=k2: e.activation(out=rr[k2][:], in_=cx.ps[6][:], func=AF.Ln, scale=1.0 / 128, bias=cx.epsb[:]),
                      reads=[cx.Bps[6], cx.Beps], writes=[Brr[k2]])
                S.add("act", lambda e, k2=k2: e.activation(out=rr[k2][:], in_=rr[k2][:], func=AF.Exp, scale=-0.5),
                      reads=[Brr[k2]], writes=[Brr[k2]])
                S.add("dve", lambda e, k2=k2, ob=ob: e.tensor_tensor(out=o1[k2][:], in0=cx.ps[ob][:], in1=rr[k2][:], op=ALU.mult),
                      reads=[cx.Bps[ob], Brr[k2]], writes=[Bo1[k2]])
                S.add("dve", lambda e, k2=k2, sl=sl: e.tensor_tensor(out=ost[k2][:], in0=o1[k2][:], in1=gate[:, sl], op=ALU.mult),
                      reads=[Bo1[k2], Bgate[t]], writes=[Bost[k2]])
                dst = p.o_dst(h * 128, t)
                S.add("sp", lambda e, k2=k2, dst=dst: e.dma_start(out=dst, in_=ost[k2][:]),
                      reads=[Bost[k2]], writes=[Bout], dma=True)

        stage_a(0)
        for i in range(NBLK):
            if i + 1 < NBLK:
                stage_a(i + 1)
            stage_b(i)


def sb_heads(cx, p, heads, pre=None):
    S = cx.S
    with Phase(cx) as ph:
        if pre is not None:
            pre()
        def ring(name, n, dt=F32, w=TT):
            return [ph.sb("%s%d" % (name, i), [128, w], dt) for i in range(n)], [Buf() for _ in range(n)]
        zer = ph.sb("zer", [128, 128], BF16)
        Bzer = Buf()
        S.add("dve", lambda e: e.memset(zer[:], 0.0), writes=[Bzer])
        steps = [(G, J) for G in range(NT8) for J in range(4 * G + 3, -1, -1)]
        N = len(steps)

        def geom(n):
            G, J = steps[n]
            lo = max(J - 4 * G, 0) * 128
            return G, J, lo, J >= 4 * G, slice(lo, TT), slice(J * 128, (J + 1) * 128)

        NH = len(heads)
        sp_all = [ph.sb("spall%d" % i, [128, NH * TT], F32) for i in range(3)]
        tm_all = [ph.sb("tmall%d" % i, [128, NH * TT], F32) for i in range(2)]
        at_all = [ph.sb("atall%d" % i, [128, NH * TT], BF16) for i in range(2)]

        def both(t, cs):
            return t[:].rearrange("p (h t) -> p h t", h=NH)[:, :, cs]
        chains = []
        for ci, h in enumerate(heads):
            c = P2Ctx()
            c.h = h
            c.b0 = 4 * ci
            c.qT = ph.sb("qT%d" % ci, [128, SEQ], BF16)
            c.kT = ph.sb("kT%d" % ci, [128, SEQ], BF16)
            c.vtok = ph.sb("vtokb%d" % ci, [128, SEQ], BF16)
            c.Bq = [Buf() for _ in range(NT8)]
            c.Bk = [Buf() for _ in range(NT8)]
            c.Bv = [Buf() for _ in range(NT8)]
            hs = slice(ci * TT, (ci + 1) * TT)
            c.sp, c.Bsp = [t[:, hs] for t in sp_all], [Buf() for _ in range(3)]
            c.Lb, c.BLb = ring("Lb%d_" % ci, 3, BF16)
            c.tm, c.Btm = [t[:, hs] for t in tm_all], [Buf() for _ in range(2)]
            c.at, c.Bat = [t[:, hs] for t in at_all], [Buf() for _ in range(2)]
            c.ost, c.Bost = ring("ostb%d_" % ci, 2, BF16)
            c.Bout = Buf()
            chains.append(c)
        for ci, c in enumerate(chains):
            p.head_bufs(ph, *_head_cols(True, c.h), new_ring=(ci == 0))
            for t in range(NT8):
                sl = slice(t * TT, (t + 1) * TT)
                ht, Bht = p.load_h(t)
                _proj_fm(cx, p, c.b0, _col(4, c.h), ht, Bht)
                S.add("act", lambda e, sl=sl, c=c: e.activation(out=c.qT[:, sl], in_=cx.ps[c.b0][:], func=AF.Copy),
                      reads=[cx.Bps[c.b0]], writes=[c.Bq[t]])
                _proj_fm(cx, p, c.b0 + 1, _col(5, c.h), ht, Bht)
                S.add("dve", lambda e, sl=sl, c=c: e.tensor_copy(out=c.kT[:, sl], in_=cx.ps[c.b0 + 1][:]),
                      reads=[cx.Bps[c.b0 + 1]], writes=[c.Bk[t]])
                _proj_tm(cx, p, c.b0 + 2, _col(6, c.h), ht, Bht)
                S.add("act", lambda e, sl=sl, c=c: e.activation(out=c.vtok[:, sl], in_=cx.ps[c.b0 + 2][:], func=AF.Copy),
                      reads=[cx.Bps[c.b0 + 2]], writes=[c.Bv[t]])

        def a_pe(c, n):
            G, J, lo, diag, cs, js = geom(n)
            zb = c.b0 + n % 2
            mm(cx, cx.ps[zb][:, cs], c.kT[:, js], c.qT[:, G * TT + lo:(G + 1) * TT], True, True, [c.Bk[J // 4], c.Bq[G]], [cx.Bps[zb]])

        def a_exp(c, n):
            G, J, lo, diag, cs, js = geom(n)
            zb, k = c.b0 + n % 2, n % 3
            S.add("act", lambda e: e.activation(out=c.sp[k][:, cs], in_=cx.ps[zb][:, cs], func=AF.Exp, scale=-SB_SCALE),
                  reads=[cx.Bps[zb]], writes=[c.Bsp[k]])

        def a_ln(n):
            G, J, lo, diag, cs, js = geom(n)
            k = n % 3
            Bs = [c.Bsp[k] for c in chains]
            S.add("act", lambda e: e.activation(out=both(sp_all[k], cs), in_=both(sp_all[k], cs), func=AF.Ln, bias=cx.oneb[:]),
                  reads=Bs + [cx.Beps], writes=Bs)

        def a_dve(c, n):
            G, J, lo, diag, cs, js = geom(n)
            zb, k = c.b0 + n % 2, n % 3
            S.add("dve", lambda e: e.scalar_tensor_tensor(out=c.Lb[k][:, cs], in0=cx.ps[zb][:, cs], scalar=-SB_SCALE,
                                                          in1=c.sp[k][:, cs], op0=ALU.mult, op1=ALU.subtract),
                  reads=[cx.Bps[zb], c.Bsp[k]], writes=[c.BLb[k]])
            if diag:
                ds_ = slice(lo, lo + 128)
                S.add("dve", lambda e: e.tensor_tensor(out=c.Lb[k][:, ds_], in0=c.Lb[k][:, ds_], in1=p.Lm[:], op=ALU.mult),
                      reads=[c.BLb[k], p.BLm], writes=[c.BLb[k]])

        def b_open(c, n):
            G, J, lo, diag, cs, js = geom(n)
            if J == 4 * G + 3:
                for b in (c.b0 + 2, c.b0 + 3):
                    mm(cx, cx.ps[b][:, :], zer[:], c.qT[:, G * TT:(G + 1) * TT], True, True, [Bzer, c.Bq[G]], [cx.Bps[b]], skip=True)

        def b_u(c, n):
            G, J, lo, diag, cs, js = geom(n)
            k, b = n % 3, c.b0 + 2
            mm(cx, cx.ps[b][:, cs], p.U[:], c.Lb[k][:, cs], False, True, [p.BU, c.BLb[k]], [cx.Bps[b]], skip=True)

        def b_tm(c, n):
            G, J, lo, diag, cs, js = geom(n)
            k, k2, b = n % 3, n % 2, c.b0 + 2
            S.add("dve", lambda e: e.tensor_tensor(out=c.tm[k2][:, cs], in0=cx.ps[b][:, cs], in1=c.sp[k][:, cs], op=ALU.subtract),
                  reads=[cx.Bps[b], c.Bsp[k]], writes=[c.Btm[k2]])
            if diag:
                ds_ = slice(lo, lo + 128)
                S.add("dve", lambda e: e.tensor_tensor(out=c.tm[k2][:, ds_], in0=c.tm[k2][:, ds_], in1=p.mb[:], op=ALU.add),
                      reads=[c.Btm[k2], p.Bmb], writes=[c.Btm[k2]])

        def b_exp(n):
            G, J, lo, diag, cs, js = geom(n)
            k2 = n % 2
            S.add("act", lambda e: e.activation(out=both(at_all[k2], cs), in_=both(tm_all[k2], cs), func=AF.Exp),
                  reads=[c.Btm[k2] for c in chains], writes=[c.Bat[k2] for c in chains])

        def b_v(c, n):
            G, J, lo, diag, cs, js = geom(n)
            k, b = n % 3, c.b0 + 2
            mm(cx, cx.ps[b][:, cs], p.V[:], c.Lb[k][:, cs], False, True, [p.BV, c.BLb[k]], [cx.Bps[b]], skip=True)

        def b_av(c, n):
            G, J, lo, diag, cs, js = geom(n)
            k2, b = n % 2, c.b0 + 3
            mm(cx, cx.ps[b][:, cs], c.vtok[:, js], c.at[k2][:, cs], False, True, [c.Bv[J // 4], c.Bat[k2]], [cx.Bps[b]], skip=True)
            if J == 0:
                ko = G % 2
                S.add("act", lambda e: e.activation(out=c.ost[ko][:], in_=cx.ps[b][:], func=AF.Copy), reads=[cx.Bps[b]], writes=[c.Bost[ko]])
                dst = p.o_dst(256 + c.h * 128, G)
                S.add("sp", lambda e: e.dma_start(out=dst, in_=c.ost[ko][:]), reads=[c.Bost[ko]], writes=[c.Bout], dma=True)

        def each(fn, n):
            for c in chains:
                fn(c, n)

        for n0 in range(min(2, N)):
            each(a_pe, n0)
            each(a_exp, n0)
            a_ln(n0)
            each(a_dve, n0)
        for n in range(N):
            G_, J_ = steps[n]
            first = (J_ == 4 * G_ + 3)
            if n >= 1 and first:
                each(b_av, n - 1)
            each(b_open, n)
            each(b_u, n)
            each(b_tm, n)
            if n >= 1 and not first:
                each(b_av, n - 1)
            if n + 2 < N:
                each(a_pe, n + 2)
                each(a_exp, n + 2)
                a_ln(n + 2)
            b_exp(n)
            each(b_v, n)
            if n + 2 < N:
                each(a_dve, n + 2)
        each(b_av, N - 1)


P2_PARTS = ("a0", "a1", "b0", "b1")


def build_p2(e_idx):
    nc = bass.Bass("TRN2", target_bir_lowering=False)
    hT_d = _dram_in(nc, "hT_full", [D, SEQ], BF16)
    w_d = _dram_in(nc, "w_in", [D, WSEL])
    lbl_d = _dram_in(nc, "lbl", [2, 256])
    gn_d = _dram_in(nc, "gn", [256])
    oT_d = _dram_out(nc, "oT", [512, SEQ], BF16)
    cx = Ctx(nc, with_x=False)
    p = p2_consts(cx, nc)
    hv = hT_d.rearrange("(c p) n -> p c n", p=128)
    p2_layer(cx, p, e_idx, "", lambda t: hv[:, :, t * TT:(t + 1) * TT], Buf(), w_d, lbl_d, gn_d)
    p.o_dst = lambda row0, t: oT_d[row0:row0 + 128, t * TT:(t + 1) * TT]
    for h in range(2):
        if "a%d" % h in P2_PARTS:
            hgrn_head(cx, p, h, oT_d)
    bh = [h for h in range(2) if "b%d" % h in P2_PARTS]
    if bh:
        sb_heads(cx, p, bh)
    with Phase(cx):
        pass
    return nc


def build_p1():
    nc = bass.Bass("TRN2", target_bir_lowering=False)
    xin = _dram_in(nc, "xT_in", [D, TOK])
    g = _dram_in(nc, "mixg", [D])
    hout = _dram_out(nc, "hT", [D, TOK], BF16)
    cx = Ctx(nc)
    gT, Bg = cx.vec("mixg_sb", g, DC)
    load_xT(cx, xin)
    p1_phase(cx, gT, Bg, hout)
    return nc


def build_mid(final):
    nc = bass.Bass("TRN2", target_bir_lowering=False)
    xin = _dram_in(nc, "xT_in", [D, TOK])
    oT = _dram_in(nc, "oT_in", [D, TOK], BF16)
    wout = _dram_in(nc, "ab_wout", [D, D])
    mlpg = _dram_in(nc, "e_mlpg", [D])
    w1 = _dram_in(nc, "e_w1", [D, HID])
    w2 = _dram_in(nc, "e_w2", [HID, D])
    ng = _dram_in(nc, "next_g", [D])
    cx = Ctx(nc)
    mgT, Bmg = cx.vec("e_mlpg_sb", mlpg, DC)
    ngT, Bng = cx.vec("next_g_sb", ng, DC)
    load_xT(cx, xin)
    p3_phase(cx, oT, wout)
    mlp_layer(cx, mgT, Bmg, w1, w2)
    odd_layer(cx, nc, "o_")
    if final:
        out = _dram_out(nc, "outT", [D, TOK])
        final_phase(cx, ngT, Bng, out)
    else:
        xout = _dram_out(nc, "xT_out", [D, TOK])
        hout = _dram_out(nc, "hT", [D, TOK], BF16)
        with Phase(cx):
            store_T(cx, xout, cx.xT, cx.Bx)
        p1_phase(cx, ngT, Bng, hout)
    return nc


def p3_fused(cx, o_all, wout_dram, selT, Bsel):
    S = cx.S
    with Phase(cx) as ph:
        oA = ph.sb("oA", [128, DC, TOK], BF16)
        Bo = [Buf() for _ in range(NTT)]
        oB = [ph.sb("oB%d" % i, [128, DC, TT], BF16) for i in range(2)]
        BoB = [Buf(), Buf()]
        w = ph.sb("wout", [128, DC, D], BF16)
        Bw = Buf()
        ovA = o_all[0].rearrange("(c p) n -> p c n", p=128)
        ovB = o_all[1].rearrange("(c p) n -> p c n", p=128)
        S.add("pool", lambda e: e.dma_start(out=w[:], in_=wout_dram.rearrange("(c p) o -> p c o", p=128)),
              writes=[Bw], dma=True)
        for t in range(NTT):
            sl = slice(t * TT, (t + 1) * TT)
            sl1 = slice(TOK + t * TT, TOK + (t + 1) * TT)
            k = t % 2
            for half, ov in ((0, ovA), (1, ovB)):
                cs4 = slice(half * 4, half * 4 + 4)
                S.add("sp", lambda e, sl=sl, ov=ov, cs4=cs4: e.dma_start(out=oA[:, cs4, sl], in_=ov[:, :, sl]), writes=[Bo[t]], dma=True)
                S.add("sp", lambda e, sl1=sl1, ov=ov, cs4=cs4, k=k: e.dma_start(out=oB[k][:, cs4, :], in_=ov[:, :, sl1]),
                      writes=[BoB[k]], dma=True)
            for c in range(DC):
                S.add("dve", lambda e, c=c, sl=sl: e.tensor_scalar(out=oA[:, c, sl], in0=oA[:, c, sl], scalar1=selT[:, 0:1],
                                                                   scalar2=None, op0=ALU.mult), reads=[Bo[t], Bsel], writes=[Bo[t]])
                S.add("dve", lambda e, c=c, sl=sl, k=k: e.scalar_tensor_tensor(
                    out=oA[:, c, sl], in0=oB[k][:, c, :], scalar=selT[:, 1:2], in1=oA[:, c, sl], op0=ALU.mult, op1=ALU.add),
                    reads=[BoB[k], Bo[t], Bsel], writes=[Bo[t]])
            for oc in range(DC):
                bank = oc % 4
                for kc in range(DC):
                    mm(cx, cx.ps[bank][:], w[:, kc, oc * 128:(oc + 1) * 128], oA[:, kc, sl], kc == 0, kc == DC - 1,
                       [Bw, Bo[t]], [cx.Bps[bank]])
                S.add("dve", lambda e, oc=oc, sl=sl, bank=bank: e.tensor_tensor(
                    out=cx.xT[:, oc, sl], in0=cx.xT[:, oc, sl], in1=cx.ps[bank][:], op=ALU.add),
                    reads=[cx.Bps[bank], cx.Bx[t]], writes=[cx.Bx[t]])


PAIRS = [[0, 1], [2, 3], [4, 5], [6, 7]]


def build_fused():
    nc = bass.Bass("TRN2", target_bir_lowering=False)
    xin = _dram_in(nc, "xT_in", [D, TOK])
    sel_d = _dram_in(nc, "sel", [128, 2])
    out = _dram_out(nc, "outT", [D, TOK])
    fg = _dram_in(nc, "final_g", [D])
    HT = TOK // 2
    h_src = [nc.dram_tensor("h_src%d" % k, [D, HT], BF16) for k in range(2)]
    h_all = [nc.dram_tensor("h_all%d" % k, [2 * D, HT], BF16) for k in range(2)]
    o_src = [nc.dram_tensor("o_src%d" % k, [256, SEQ], BF16) for k in range(2)]
    o_all = [nc.dram_tensor("o_all%d" % k, [512, SEQ], BF16) for k in range(2)]
    cx = Ctx(nc)
    S = cx.S
    p = p2_consts(cx, nc)
    selT = nc.alloc_sbuf_tensor("sel_sb", [128, 2], F32)
    Bsel = Buf()
    S.add("sp", lambda e: e.dma_start(out=selT[:], in_=sel_d), writes=[Bsel], dma=True)
    fgT, Bfg = cx.vec("final_g_sb", fg, DC)
    load_xT(cx, xin)
    def h_dst(t):
        return h_src[t // 2].ap()[:, (t % 2) * TT:(t % 2 + 1) * TT].rearrange("(c p) n -> p c n", p=128)

    def h_tile(t):
        r, q = t // 4, t % 4
        return h_all[q // 2].ap()[r * D:(r + 1) * D, (q % 2) * TT:(q % 2 + 1) * TT].rearrange("(c p) n -> p c n", p=128)

    def o_dst(row0, t):
        return o_src[row0 // 256].ap()[row0 % 256:row0 % 256 + 128, t * TT:(t + 1) * TT]
    p.o_dst = o_dst

    def gather_one(src, dst):
        S.add("pool", lambda e_: e_.collective_compute("AllGather", ALU.bypass, replica_groups=PAIRS,
                                                       ins=[src.ap().opt()], outs=[dst.ap().opt()]), cc=True)

    def gather(src, dst):
        with Phase(cx):
            for k in range(2):
                S.add("pool", lambda e_, k=k: e_.collective_compute("AllGather", ALU.bypass, replica_groups=PAIRS,
                                                                    ins=[src[k].ap().opt()], outs=[dst[k].ap().opt()]), cc=True)

    for e in range(2):
        tag = "e%d_" % e
        mixg = _dram_in(nc, tag + "mixg", [D])
        w_in = _dram_in(nc, tag + "w_in", [D, WSEL])
        lbl = _dram_in(nc, tag + "lbl", [2, 256])
        gn = _dram_in(nc, tag + "gn", [256])
        wout = _dram_in(nc, tag + "wout", [D, D])
        mlpg = _dram_in(nc, tag + "mlpg", [D])
        w1 = _dram_in(nc, tag + "w1", [D, HID])
        w2 = _dram_in(nc, tag + "w2", [HID, D])
        gT, Bg = cx.vec(tag + "mixg_sb", mixg, DC)
        mgT, Bmg = cx.vec(tag + "mlpg_sb", mlpg, DC)
        p1_phase(cx, gT, Bg, None, dst_fn=h_dst)
        gather(h_src, h_all)
        p2_layer(cx, p, e, tag, h_tile, Buf(), w_in, lbl, gn)
        for h in range(2):
            hgrn_head(cx, p, h, None)
        sb_heads(cx, p, [0, 1], pre=lambda: gather_one(o_src[0], o_all[0]))
        with Phase(cx):
            gather_one(o_src[1], o_all[1])
        p3_fused(cx, [o_all[0].ap(), o_all[1].ap()], wout, selT, Bsel)
        od = odd_decl(nc, "o%d_" % e)
        sc = gmlp_scratch(cx)
        mlp_layer(cx, mgT, Bmg, w1, w2, extra=gmlp_precast_ops(cx, sc, od.win, od.wout))
        odd_run(cx, od, pre=sc)
    final_phase(cx, fgT, Bfg, out)
    return nc


def _run(nc, in_maps):
    res = run_bass_kernel_spmd(nc, in_maps, core_ids=list(range(NCORES)))
    return res.results


def _c(a):
    return np.ascontiguousarray(a)


def kernel_unfused(x, mix_norm, mlp_norm, mlp_w1, mlp_w2, ab_w_in, ab_w_out, hgrn_lb_logits, hgrn_out_norm,
           gm_w_in, gm_ln_g, gm_ln_b, gm_ws, gm_bs, gm_w_out, final_norm):
    f32 = lambda a: np.asarray(a, dtype=np.float32)
    x, mix_norm, mlp_norm, mlp_w1, mlp_w2 = f32(x), f32(mix_norm), f32(mlp_norm), f32(mlp_w1), f32(mlp_w2)
    ab_w_in, ab_w_out, hgrn_lb_logits, hgrn_out_norm = f32(ab_w_in), f32(ab_w_out), f32(hgrn_lb_logits), f32(hgrn_out_norm)
    gm_w_in, gm_ln_g, gm_ln_b, gm_ws, gm_bs, gm_w_out, final_norm = (f32(gm_w_in), f32(gm_ln_g), f32(gm_ln_b), f32(gm_ws),
                                                                        f32(gm_bs), f32(gm_w_out), f32(final_norm))
    cores = range(NCORES)
    xT = [_c(x[c // 2, (c % 2) * TOK:(c % 2 + 1) * TOK].T) for c in cores]

    def wsel(e, g):
        cols = []
        for kinds in (range(0, 4), range(4, 7)):
            for h in range(2):
                for kind in kinds:
                    c0 = kind * 512 + (2 * g + h) * 128
                    cols.append(ab_w_in[e][:, c0:c0 + 128])
        return _c(np.concatenate(cols, axis=1))

    def mixers(e, hT):
        hfull = [_c(np.concatenate([hT[2 * b], hT[2 * b + 1]], axis=1)) for b in range(4)]
        ins = [{"hT_full": hfull[c // 2], "w_in": wsel(e, c % 2),
                "lbl": _c(hgrn_lb_logits[:, (c % 2) * 256:(c % 2 + 1) * 256]),
                "gn": _c(hgrn_out_norm[e][(c % 2) * 256:(c % 2 + 1) * 256])} for c in cores]
        r = _run(build_p2(e), ins)
        outs = []
        for c in cores:
            b, half = c // 2, c % 2
            o0, o1 = r[2 * b]["oT"], r[2 * b + 1]["oT"]
            full = np.concatenate([o0[0:256], o1[0:256], o0[256:512], o1[256:512]], axis=0)
            outs.append(_c(full[:, half * TOK:(half + 1) * TOK]))
        return outs

    def odd_inputs(o, L):
        return {"o_mixg": mix_norm[L], "o_gwin": gm_w_in[o], "o_lng": gm_ln_g[o], "o_lnb": gm_ln_b[o],
                "o_wsT": _c(gm_ws[o].transpose(0, 2, 1)), "o_bs": gm_bs[o], "o_gwout": gm_w_out[o],
                "o_mlpg": mlp_norm[L], "o_w1": mlp_w1[L], "o_w2": mlp_w2[L]}

    r0 = _run(build_p1(), [{"xT_in": xT[c], "mixg": mix_norm[0]} for c in cores])
    oT = mixers(0, [r0[c]["hT"] for c in cores])
    common = dict(odd_inputs(0, 1), ab_wout=ab_w_out[0], e_mlpg=mlp_norm[0], e_w1=mlp_w1[0], e_w2=mlp_w2[0], next_g=mix_norm[2])
    r2 = _run(build_mid(False), [dict(common, xT_in=xT[c], oT_in=oT[c]) for c in cores])
    oT = mixers(1, [r2[c]["hT"] for c in cores])
    common = dict(odd_inputs(1, 3), ab_wout=ab_w_out[1], e_mlpg=mlp_norm[2], e_w1=mlp_w1[2], e_w2=mlp_w2[2], next_g=final_norm)
    r4 = _run(build_mid(True), [dict(common, xT_in=_c(r2[c]["xT_out"]), oT_in=oT[c]) for c in cores])
    out = np.empty((4, SEQ, D), dtype=np.float32)
    for c in cores:
        out[c // 2, (c % 2) * TOK:(c % 2 + 1) * TOK] = r4[c]["outT"].T
    return out


def kernel(x, mix_norm, mlp_norm, mlp_w1, mlp_w2, ab_w_in, ab_w_out, hgrn_lb_logits, hgrn_out_norm,
           gm_w_in, gm_ln_g, gm_ln_b, gm_ws, gm_bs, gm_w_out, final_norm):
    f32 = lambda a: np.asarray(a, dtype=np.float32)
    x, mix_norm, mlp_norm, mlp_w1, mlp_w2 = f32(x), f32(mix_norm), f32(mlp_norm), f32(mlp_w1), f32(mlp_w2)
    ab_w_in, ab_w_out, hgrn_lb_logits, hgrn_out_norm = f32(ab_w_in), f32(ab_w_out), f32(hgrn_lb_logits), f32(hgrn_out_norm)
    gm_w_in, gm_ln_g, gm_ln_b, gm_ws, gm_bs, gm_w_out, final_norm = (f32(gm_w_in), f32(gm_ln_g), f32(gm_ln_b), f32(gm_ws),
                                                                        f32(gm_bs), f32(gm_w_out), f32(final_norm))
    cores = range(NCORES)

    def wsel(e, g):
        cols = []
        for kinds in (range(0, 4), range(4, 7)):
            for h in range(2):
                for kind in kinds:
                    c0 = kind * 512 + (2 * g + h) * 128
                    cols.append(ab_w_in[e][:, c0:c0 + 128])
        return _c(np.concatenate(cols, axis=1))

    shared = {"final_g": final_norm}
    for e in range(2):
        L, tag = 2 * e, "e%d_" % e
        shared.update({tag + "mixg": mix_norm[L], tag + "wout": ab_w_out[e], tag + "mlpg": mlp_norm[L],
                       tag + "w1": mlp_w1[L], tag + "w2": mlp_w2[L]})
        o, Lo, t2 = e, 2 * e + 1, "o%d_" % e
        shared.update({t2 + "mixg": mix_norm[Lo], t2 + "gwin": gm_w_in[o], t2 + "lng": gm_ln_g[o], t2 + "lnb": gm_ln_b[o],
                       t2 + "wsT": _c(gm_ws[o].transpose(0, 2, 1)), t2 + "bs": gm_bs[o], t2 + "gwout": gm_w_out[o],
                       t2 + "mlpg": mlp_norm[Lo], t2 + "w1": mlp_w1[Lo], t2 + "w2": mlp_w2[Lo]})
    per_rank = []
    for g in range(2):
        d = {"sel": _c(np.tile(np.eye(2, dtype=np.float32)[g], (128, 1)))}
        for e in range(2):
            tag = "e%d_" % e
            d.update({tag + "w_in": wsel(e, g), tag + "lbl": _c(hgrn_lb_logits[:, g * 256:(g + 1) * 256]),
                      tag + "gn": _c(hgrn_out_norm[e][g * 256:(g + 1) * 256])})
        per_rank.append(d)
    in_maps = [dict(shared, **per_rank[c % 2], xT_in=_c(x[c // 2, (c % 2) * TOK:(c % 2 + 1) * TOK].T)) for c in cores]
    r = _run(build_fused(), in_maps)
    out = np.empty((4, SEQ, D), dtype=np.float32)
    for c in cores:
        out[c // 2, (c % 2) * TOK:(c % 2 + 1) * TOK] = r[c]["outT"].T
    return out
```

```python
import contextlib
import numpy as np
import ml_dtypes
import concourse.bass as bass
import concourse.mybir as mybir
from concourse.bass_utils import run_bass_kernel_spmd

F32 = mybir.dt.float32
BF16 = mybir.dt.bfloat16
AF = mybir.ActivationFunctionType
ALU = mybir.AluOpType

D = 1024
DC = D // 128
TOK = 2048
SEQ = 4096
TT = 512
NTT = TOK // TT
HID = 4096
EPS = 1e-6
NCORES = 8


class Buf:
    __slots__ = ("name", "w", "r")

    def __init__(self, name=""):
        self.name = name
        self.w = None
        self.r = []


class _Op:
    __slots__ = ("eng", "fn", "deps", "signal", "sem", "val", "dma", "retired", "inc")


_ENGS = ("pe", "act", "dve", "pool", "sp")
_ENG_ATTR = {"pe": "tensor", "act": "scalar", "dve": "vector", "pool": "gpsimd", "sp": "sync"}
_DMA_ENGS = ("sp", "pool")


class Sched:
    def __init__(self, nc, n_dma_sems=8):
        self.nc = nc
        self.ops = {e: [] for e in _ENGS}
        self.n_dma_sems = n_dma_sems
        self.dma_rr = {e: 0 for e in _ENGS}
        self.dma_last = {}
        self.cnt = {e: 0 for e in _ENGS}
        self.waited = {e: {} for e in _ENGS}
        self.sems = {e: nc.alloc_semaphore("s_" + e) for e in ("pe", "act", "dve", "pool")}
        for e in _DMA_ENGS:
            for k in range(n_dma_sems):
                self.sems[(e, k)] = nc.alloc_semaphore("d_%s_%d" % (e, k))
        self.sems[("cc", 0)] = nc.alloc_semaphore("cc_sem")
        self.nops = 0

    def add(self, eng, fn, reads=(), writes=(), dma=False, cc=False):
        op = _Op()
        dma = dma or cc
        op.eng, op.fn, op.dma, op.signal, op.sem, op.val, op.retired = eng, fn, dma, dma, None, 0, False
        op.inc = 1 if cc else 16
        deps, seen = [], set()
        self.nops += 1

        def adddep(d):
            if d is not None and not d.retired and id(d) not in seen:
                seen.add(id(d))
                deps.append(d)

        for b in reads:
            adddep(b.w)
        for b in writes:
            adddep(b.w)
            for r in b.r:
                adddep(r)
        if dma:
            if cc:
                key = ("cc", 0)
            else:
                key = (eng, self.dma_rr[eng] % self.n_dma_sems)
                self.dma_rr[eng] += 1
            prev = self.dma_last.get(key)
            adddep(prev)
            self.dma_last[key] = op
            op.sem = key
            op.val = (prev.val if prev is not None else 0) + op.inc
        fdeps = []
        for d in deps:
            if not d.dma and d.eng == eng and eng == "pe":
                continue
            d.signal = True
            fdeps.append(d)
        op.deps = fdeps
        for b in reads:
            b.r.append(op)
        for b in writes:
            b.w = op
            b.r = []
        self.ops[eng].append(op)
        return op

    def emit(self):
        nc, sems = self.nc, self.sems
        for e in _ENGS:
            c = self.cnt[e]
            for op in self.ops[e]:
                if not op.dma and op.signal:
                    c += 1
                    op.sem, op.val = e, c
            self.cnt[e] = c
        if not any(self.ops[e] for e in _ENGS):
            return
        with nc.Block() as block:
            for e in _ENGS:
                ops = self.ops[e]
                if not ops:
                    continue

                def body(eng, ops=ops, e=e):
                    waited = self.waited[e]
                    last_dma = {}
                    for op in ops:
                        for d in op.deps:
                            if waited.get(d.sem, 0) >= d.val:
                                continue
                            eng.wait_ge(sems[d.sem], d.val)
                            waited[d.sem] = d.val
                        if op.fn is None:
                            continue
                        inst = op.fn(eng)
                        if op.signal:
                            if op.dma and op.inc == 1:
                                inst.then_inc(sems[op.sem])
                            else:
                                inst.then_inc(sems[op.sem], 16 if op.dma else 1)
                        if op.dma:
                            last_dma[op.sem] = op.val
                    for s, v in last_dma.items():
                        if waited.get(s, 0) < v:
                            eng.wait_ge(sems[s], v)
                            waited[s] = v

                getattr(block, _ENG_ATTR[e])(body)
        for e in _ENGS:
            for op in self.ops[e]:
                op.retired = True
                op.fn = None
            self.ops[e] = []


class Phase:
    def __init__(self, cx):
        self.cx = cx
        self.stack = contextlib.ExitStack()
        self.n = 0

    def __enter__(self):
        return self

    def sb(self, name, shape, dtype):
        cx = self.cx
        cx.uid += 1
        return self.stack.enter_context(cx.nc.sbuf_tensor("%s_%d" % (name, cx.uid), list(shape), dtype))

    def __exit__(self, et, ev, tb):
        if et is None:
            self.cx.S.emit()
        self.stack.close()
        return False


class Ctx:
    def __init__(self, nc, with_x=True):
        self.nc = nc
        self.S = Sched(nc)
        self.uid = 0
        S = self.S
        if with_x:
            self.xT = nc.alloc_sbuf_tensor("xT", [128, DC, TOK], F32)
            self.Bx = [Buf("x%d" % t) for t in range(NTT)]
        self.ones = nc.alloc_sbuf_tensor("ones", [128, 128], F32)
        self.ones_bf = nc.alloc_sbuf_tensor("ones_bf", [128, 128], BF16)
        self.Bones = Buf("ones")
        S.add("dve", lambda e: e.memset(self.ones[:], 1.0), writes=[self.Bones])
        S.add("dve", lambda e: e.memset(self.ones_bf[:], 1.0), writes=[self.Bones])
        self.epsb = nc.alloc_sbuf_tensor("epsb", [128, 1], F32)
        self.Beps = Buf("eps")
        S.add("dve", lambda e: e.memset(self.epsb[:], EPS), writes=[self.Beps])
        self.oneb = nc.alloc_sbuf_tensor("oneb", [128, 1], F32)
        S.add("dve", lambda e: e.memset(self.oneb[:], 1.0), writes=[self.Beps])
        self.ps = [nc.alloc_psum_tensor("ps%d" % i, [128, TT], F32) for i in range(8)]
        self.Bps = [Buf("ps%d" % i) for i in range(8)]

    def vec(self, name, dram_vec, nchunks, eng="sp"):
        t = self.nc.alloc_sbuf_tensor(name, [128, nchunks], F32)
        B = Buf(name)
        self.S.add(eng, lambda e: e.dma_start(out=t[:], in_=dram_vec.rearrange("(c p) -> p c", p=128),
                                              allow_slow_non_contiguous=True), writes=[B], dma=True)
        return t, B


def mm(cx, out, lhsT, rhs, start, stop, reads, writes, skip=False):
    cx.S.add("pe", lambda e: e.matmul(out, lhsT=lhsT, rhs=rhs, start=start, stop=stop, skip_group_check=skip),
             reads=reads, writes=writes)


def load_xT(cx, x_dram):
    v = x_dram.rearrange("(c p) n -> p c n", p=128)
    for t in range(NTT):
        sl = slice(t * TT, (t + 1) * TT)
        cx.S.add("sp", lambda e, sl=sl: e.dma_start(out=cx.xT[:, :, sl], in_=v[:, :, sl]),
                 writes=[cx.Bx[t]], dma=True)


def store_T(cx, out_dram, src, Bsrc, ncols=TOK):
    v = out_dram.rearrange("(c p) n -> p c n", p=128)
    Bout = Buf("out")
    for t in range(ncols // TT):
        sl = slice(t * TT, (t + 1) * TT)
        cx.S.add("sp", lambda e, sl=sl: e.dma_start(out=v[:, :, sl], in_=src[:, :, sl]),
                 reads=[Bsrc[t]], writes=[Bout], dma=True)
    cx.S.add("sp", None, reads=[Bout])


def rmsnorm_tile(cx, ph_bufs, t, gT, Bg, dst, Bdst_t, dsl=None):
    S = cx.S
    sq, Bsq, rstd, Brstd = ph_bufs
    sl = slice(t * TT, (t + 1) * TT)
    dsl = sl if dsl is None else dsl
    k = t % 2
    bank = t % 2
    for c in range(DC):
        kq = c % 2
        S.add("act", lambda e, c=c, kq=kq: e.activation(out=sq[kq][:], in_=cx.xT[:, c, sl], func=AF.Square),
              reads=[cx.Bx[t]], writes=[Bsq[kq]])
        mm(cx, cx.ps[bank][:], cx.ones[:], sq[kq][:], c == 0, c == DC - 1, [Bsq[kq], cx.Bones], [cx.Bps[bank]])
    S.add("act", lambda e: e.activation(out=rstd[k][:], in_=cx.ps[bank][:], func=AF.Ln, scale=1.0 / D, bias=cx.epsb[:]),
          reads=[cx.Bps[bank], cx.Beps], writes=[Brstd[k]])
    S.add("act", lambda e: e.activation(out=rstd[k][:], in_=rstd[k][:], func=AF.Exp, scale=-0.5),
          reads=[Brstd[k]], writes=[Brstd[k]])
    for c in range(DC):
        S.add("dve", lambda e, c=c: e.scalar_tensor_tensor(
            out=dst[:, c, dsl], in0=cx.xT[:, c, sl], scalar=gT[:, c:c + 1], in1=rstd[k][:],
            op0=ALU.mult, op1=ALU.mult), reads=[cx.Bx[t], Brstd[k], Bg], writes=[Bdst_t])


def norm_bufs(ph):
    sq = [ph.sb("sq%d" % i, [128, TT], F32) for i in range(2)]
    rstd = [ph.sb("rstd%d" % i, [128, TT], F32) for i in range(2)]
    return sq, [Buf("sq0"), Buf("sq1")], rstd, [Buf("rs0"), Buf("rs1")]


HP = 512
NHP = HID // HP
HPC = HP // 128


def mlp_layer(cx, gT, Bg, w1_dram, w2_dram, extra=None):
    S = cx.S
    w1v = w1_dram.rearrange("(c p) h -> p c h", p=128)
    w2v = w2_dram.rearrange("(c p) o -> p c o", p=128)
    with Phase(cx) as ph:
        hT = ph.sb("hT", [128, DC, TOK], BF16)
        Bh = [Buf("h%d" % t) for t in range(NTT)]
        nb = norm_bufs(ph)
        w1 = [ph.sb("w1_%d" % i, [128, DC, HP], BF16) for i in range(2)]
        w2 = [ph.sb("w2_%d" % i, [128, HPC, D], BF16) for i in range(2)]
        Bw1, Bw2 = [Buf(), Buf()], [Buf(), Buf()]
        hid = [ph.sb("hid%d" % i, [128, HPC, TT], BF16) for i in range(2)]
        Bhid = [Buf(), Buf()]
        relu = [ph.sb("relu%d" % i, [128, TT], F32) for i in range(2)]
        Brelu = [Buf(), Buf()]
        def w1_stage(j, t):
            wb = j % 2
            sl = slice(t * TT, (t + 1) * TT)
            hb = t % 2
            if t == 0:
                S.add("pool", lambda e: e.dma_start(out=w1[wb][:], in_=w1v[:, :, j * HP:(j + 1) * HP]),
                      writes=[Bw1[wb]], dma=True)
                S.add("pool", lambda e: e.dma_start(out=w2[wb][:], in_=w2v[:, j * HPC:(j + 1) * HPC, :]),
                      writes=[Bw2[wb]], dma=True)
                if extra:
                    for _ in range(min(3, len(extra))):
                        extra.pop(0)()
            if j == 0:
                rmsnorm_tile(cx, nb, t, gT, Bg, hT, Bh[t])
            for hc in range(HPC):
                bank = 2 + (hc % 2)
                for kc in range(DC):
                    mm(cx, cx.ps[bank][:], w1[wb][:, kc, hc * 128:(hc + 1) * 128], hT[:, kc, sl],
                       kc == 0, kc == DC - 1, [Bw1[wb], Bh[t]], [cx.Bps[bank]])
                rb = hc % 2
                S.add("act", lambda e, bank=bank, rb=rb: e.activation(out=relu[rb][:], in_=cx.ps[bank][:], func=AF.Relu),
                      reads=[cx.Bps[bank]], writes=[Brelu[rb]])
                S.add("dve", lambda e, hc=hc, rb=rb: e.tensor_tensor(
                    out=hid[hb][:, hc, :], in0=relu[rb][:], in1=relu[rb][:], op=ALU.mult),
                    reads=[Brelu[rb]], writes=[Bhid[hb]])

        def w2_stage(j, t):
            wb = j % 2
            sl = slice(t * TT, (t + 1) * TT)
            hb = t % 2
            for oc in range(DC):
                bank = 4 + (oc % 3)
                for hc in range(HPC):
                    mm(cx, cx.ps[bank][:], w2[wb][:, hc, oc * 128:(oc + 1) * 128], hid[hb][:, hc, :],
                       hc == 0, hc == HPC - 1, [Bw2[wb], Bhid[hb]], [cx.Bps[bank]])
                S.add("dve", lambda e, oc=oc, bank=bank: e.tensor_tensor(
                    out=cx.xT[:, oc, sl], in0=cx.xT[:, oc, sl], in1=cx.ps[bank][:], op=ALU.add),
                    reads=[cx.Bps[bank], cx.Bx[t]], writes=[cx.Bx[t]])

        steps = [(j, t) for j in range(NHP) for t in range(NTT)]
        w1_stage(*steps[0])
        for si, st in enumerate(steps):
            if si + 1 < len(steps):
                w1_stage(*steps[si + 1])
            w2_stage(*st)


def p1_phase(cx, gT, Bg, hT_dram, dst_fn=None):
    with Phase(cx) as ph:
        hT = ph.sb("hT", [128, DC, TOK], BF16)
        Bh = [Buf("h%d" % t) for t in range(NTT)]
        nb = norm_bufs(ph)
        for t in range(NTT):
            rmsnorm_tile(cx, nb, t, gT, Bg, hT, Bh[t])
        if dst_fn is None:
            store_T(cx, hT_dram, hT, Bh)
        else:
            Bout = Buf()
            for t in range(NTT):
                sl = slice(t * TT, (t + 1) * TT)
                dst = dst_fn(t)
                cx.S.add("sp", lambda e, sl=sl, dst=dst: e.dma_start(out=dst, in_=hT[:, :, sl]), reads=[Bh[t]], writes=[Bout], dma=True)
            cx.S.add("sp", None, reads=[Bout])


def final_phase(cx, gT, Bg, out_dram):
    with Phase(cx) as ph:
        nb = norm_bufs(ph)
        for t in range(NTT):
            rmsnorm_tile(cx, nb, t, gT, Bg, cx.xT, cx.Bx[t])
        store_T(cx, out_dram, cx.xT, cx.Bx)


def p3_phase(cx, oT_dram, wout_dram):
    S = cx.S
    with Phase(cx) as ph:
        oT = ph.sb("oT", [128, DC, TOK], BF16)
        Bo = [Buf() for _ in range(NTT)]
        w = ph.sb("wout", [128, DC, D], BF16)
        Bw = Buf()
        ov = oT_dram.rearrange("(c p) n -> p c n", p=128)
        S.add("pool", lambda e: e.dma_start(out=w[:], in_=wout_dram.rearrange("(c p) o -> p c o", p=128)),
              writes=[Bw], dma=True)
        for t in range(NTT):
            sl = slice(t * TT, (t + 1) * TT)
            S.add("sp", lambda e, sl=sl: e.dma_start(out=oT[:, :, sl], in_=ov[:, :, sl]), writes=[Bo[t]], dma=True)
        for t in range(NTT):
            sl = slice(t * TT, (t + 1) * TT)
            for oc in range(DC):
                bank = oc % 4
                for kc in range(DC):
                    mm(cx, cx.ps[bank][:], w[:, kc, oc * 128:(oc + 1) * 128], oT[:, kc, sl], kc == 0, kc == DC - 1,
                       [Bw, Bo[t]], [cx.Bps[bank]])
                S.add("dve", lambda e, oc=oc, sl=sl, bank=bank: e.tensor_tensor(
                    out=cx.xT[:, oc, sl], in0=cx.xT[:, oc, sl], in1=cx.ps[bank][:], op=ALU.add),
                    reads=[cx.Bps[bank], cx.Bx[t]], writes=[cx.Bx[t]])


GF = 3072
GFC = GF // 128
GG = 8
GPC = 3


class _Scratch:
    pass


def gmlp_scratch(cx):
    sc = _Scratch()
    cx.uid += 1
    sc.win_c = cx.nc.dram_tensor("gwin_bf_%d" % cx.uid, [12, 128, DC * 512], BF16).ap()
    sc.wout_c = cx.nc.dram_tensor("gwout_bf_%d" % cx.uid, [6, 128, 4 * D], BF16).ap()
    sc.Bwin_c = [Buf() for _ in range(12)]
    sc.Bwout_c = [Buf() for _ in range(6)]
    return sc


def gmlp_precast_ops(cx, sc, win_dram, wout_dram):
    S = cx.S
    winv = win_dram.rearrange("(c p) f -> p c f", p=128)
    woutv = wout_dram.rearrange("(c p) o -> p c o", p=128)
    ops = []
    for p_ in range(12):
        ops.append(lambda p_=p_: S.add("pool", lambda e: e.dma_start(out=sc.win_c[p_].rearrange("q (c f) -> q c f", c=DC),
                                                                    in_=winv[:, :, p_ * 512:(p_ + 1) * 512]),
                                       writes=[sc.Bwin_c[p_]], dma=True))
    for q in range(6):
        ops.append(lambda q=q: S.add("pool", lambda e: e.dma_start(out=sc.wout_c[q].rearrange("q (c f) -> q c f", c=4),
                                                                  in_=woutv[:, q * 4:(q + 1) * 4, :]),
                                     writes=[sc.Bwout_c[q]], dma=True))
    return ops


def gmlp_phase(cx, gT, Bg, win_dram, lng, Blng, lnb, Blnb, wsT_dram, bs_dram, wout_dram, pre=None):
    S = cx.S
    winv = win_dram.rearrange("(c p) f -> p c f", p=128)
    woutv = wout_dram.rearrange("(c p) o -> p c o", p=128)
    with Phase(cx) as ph:
        wgf = ph.sb("wgf", [128, GG * 128], F32)
        Bwgf = Buf()
        S.add("sp", lambda e: e.dma_start(out=wgf[:].rearrange("s (g t) -> s g t", g=GG),
                                          in_=wsT_dram.rearrange("g s t -> s g t")), writes=[Bwgf], dma=True)
        S.add("pool", lambda e: e.affine_select(out=wgf[:].rearrange("s (g t) -> s g t", g=GG),
                                                in_=wgf[:].rearrange("s (g t) -> s g t", g=GG),
                                                pattern=[[0, GG], [1, 128]], compare_op=ALU.is_ge, fill=0.0,
                                                base=0, channel_multiplier=-1), reads=[Bwgf], writes=[Bwgf])
        wg = ph.sb("wg", [128, GG * 128], BF16)
        Bwg = Buf()
        S.add("dve", lambda e: e.tensor_copy(out=wg[:], in_=wgf[:]), reads=[Bwgf], writes=[Bwg])
        bsrow = ph.sb("bsrow", [1, GG * 128], F32)
        Bbsrow = Buf()
        S.add("sp", lambda e: e.dma_start(out=bsrow[:], in_=bs_dram.rearrange("(o g) t -> o (g t)", o=1)),
              writes=[Bbsrow], dma=True)
        bsb = ph.sb("bsb", [128, GG * 128], F32)
        Bbsb = Buf()
        C2 = ph.sb("C2", [128, GFC * 128], F32)
        BC2 = Buf()
        for half in range(2):
            cs = slice(half * 512, (half + 1) * 512)
            mm(cx, cx.ps[5][:], cx.ones[0:1, :], bsrow[0:1, cs], True, True, [Bbsrow, cx.Bones], [cx.Bps[5]])
            S.add("act", lambda e, cs=cs: e.activation(out=bsb[:, cs], in_=cx.ps[5][:], func=AF.Copy),
                  reads=[cx.Bps[5]], writes=[Bbsb])
            mm(cx, cx.ps[6][:], cx.ones[:], wgf[:, cs], True, True, [Bwgf, cx.Bones], [cx.Bps[6]])
            for gl in range(4):
                g = half * 4 + gl
                for j in range(GPC):
                    fc = g * GPC + j
                    S.add("dve", lambda e, gl=gl, g=g, fc=fc: e.scalar_tensor_tensor(
                        out=C2[:, fc * 128:(fc + 1) * 128], in0=cx.ps[6][:, gl * 128:(gl + 1) * 128],
                        scalar=lnb[:, fc:fc + 1], in1=bsb[:, g * 128:(g + 1) * 128], op0=ALU.mult, op1=ALU.add),
                        reads=[cx.Bps[6], Bbsb, Blnb], writes=[BC2])
        hTg = ph.sb("hTg", [128, DC, TT], BF16)
        Bhg = Buf()
        nb = norm_bufs(ph)
        NWIN = 3
        win = [ph.sb("win%d" % i, [128, DC, 512], BF16) for i in range(NWIN)]
        Bwin = [Buf() for _ in range(NWIN)]
        wout = [ph.sb("wo%d" % i, [128, 4, D], BF16) for i in range(2)]
        Bwout = [Buf(), Buf()]
        uT = ph.sb("uT", [128, GFC, TT], BF16)
        Bu = [Buf() for _ in range(GFC)]
        vt = ph.sb("vt", [128, 4, GF], BF16)
        Bv = [Buf() for _ in range(4)]
        tmpf = [ph.sb("tmpf%d" % i, [128, 512], F32) for i in range(2)]
        Btmpf = [Buf(), Buf()]
        stats = ph.sb("stats", [128, 4 * 6 * 6], F32)
        Bst = [Buf() for _ in range(4)]
        mv = ph.sb("mv", [128, 8], F32)
        Bmv = Buf()
        lrs = ph.sb("lrs", [128, 4], F32)
        Blrs = Buf()
        tmpm = [wgf[:, i * 512:(i + 1) * 512] for i in range(2)]
        Btmpm = [Bwgf, Bwgf]
        wstep = 0
        ostep = 0
        if pre is None:
            pre_done = False
            sc = gmlp_scratch(cx)
        else:
            pre_done = True
            sc = pre
        win_c, wout_c, Bwin_c, Bwout_c = sc.win_c, sc.wout_c, sc.Bwin_c, sc.Bwout_c
        for tg in range(NTT):
            sl = slice(tg * TT, (tg + 1) * TT)
            rmsnorm_tile(cx, nb, tg, gT, Bg, hTg, Bhg, dsl=slice(0, TT))
            nv = 0
            for p in range(12):
                wb = wstep % NWIN
                wstep += 1
                if tg == 0 and not pre_done:
                    S.add("pool", lambda e, p=p, wb=wb: e.dma_start(out=win[wb][:], in_=winv[:, :, p * 512:(p + 1) * 512]),
                          writes=[Bwin[wb]], dma=True)
                    S.add("sp", lambda e, p=p, wb=wb: e.dma_start(out=win_c[p].rearrange("q (c f) -> q c f", c=DC), in_=win[wb][:]),
                          reads=[Bwin[wb]], writes=[Bwin_c[p]], dma=True)
                else:
                    S.add("sp", lambda e, p=p, wb=wb: e.dma_start(out=win[wb][:], in_=win_c[p].rearrange("q (c f) -> q c f", c=DC)),
                          reads=[Bwin_c[p]], writes=[Bwin[wb]], dma=True)
                if p < 6:
                    for j in range(4):
                        fc = p * 4 + j
                        bank = 2 + (j % 2)
                        for kc in range(DC):
                            mm(cx, cx.ps[bank][:], win[wb][:, kc, j * 128:(j + 1) * 128], hTg[:, kc, :],
                               kc == 0, kc == DC - 1, [Bwin[wb], Bhg], [cx.Bps[bank]])
                        S.add("act", lambda e, fc=fc, bank=bank: e.activation(out=uT[:, fc, :], in_=cx.ps[bank][:], func=AF.Gelu),
                              reads=[cx.Bps[bank]], writes=[Bu[fc]])
                else:
                    pv = p - 6
                    for ch in range(4):
                        bank = 2 + (ch % 2)
                        for kc in range(DC):
                            mm(cx, cx.ps[bank][:], hTg[:, kc, ch * 128:(ch + 1) * 128], win[wb][:, kc, :],
                               kc == 0, kc == DC - 1, [Bwin[wb], Bhg], [cx.Bps[bank]])
                        k = nv % 2
                        nv += 1
                        S.add("act", lambda e, k=k, bank=bank: e.activation(out=tmpf[k][:], in_=cx.ps[bank][:], func=AF.Gelu),
                              reads=[cx.Bps[bank]], writes=[Btmpf[k]])
                        S.add("dve", lambda e, k=k, ch=ch, pv=pv: e.bn_stats(
                            out=stats[:, (ch * 6 + pv) * 6:(ch * 6 + pv + 1) * 6], in_=tmpf[k][:]),
                            reads=[Btmpf[k]], writes=[Bst[ch]])
                        S.add("pool", lambda e, k=k, ch=ch, pv=pv: e.tensor_copy(
                            out=vt[:, ch, pv * 512:(pv + 1) * 512], in_=tmpf[k][:]),
                            reads=[Btmpf[k]], writes=[Bv[ch]])
            for ch in range(4):
                S.add("dve", lambda e, ch=ch: e.bn_aggr(out=mv[:, 2 * ch:2 * ch + 2], in_=stats[:, ch * 36:(ch + 1) * 36]),
                      reads=[Bst[ch]], writes=[Bmv])
            mv3 = mv[:].rearrange("p (c two) -> p c two", two=2)
            S.add("act", lambda e: e.activation(out=lrs[:], in_=mv3[:, :, 1], func=AF.Ln, bias=cx.epsb[:]),
                  reads=[Bmv, cx.Beps], writes=[Blrs])
            S.add("act", lambda e: e.activation(out=lrs[:], in_=lrs[:], func=AF.Exp, scale=-0.5),
                  reads=[Blrs], writes=[Blrs])
            for ch in range(4):
                S.add("dve", lambda e, ch=ch: e.tensor_scalar(
                    out=vt[:, ch, :], in0=vt[:, ch, :], scalar1=mv[:, 2 * ch:2 * ch + 1], scalar2=lrs[:, ch:ch + 1],
                    op0=ALU.subtract, op1=ALU.mult), reads=[Bv[ch], Bmv, Blrs], writes=[Bv[ch]])
            def gate_piece(q):
                for fc in range(4 * q, 4 * q + 4):
                    g = fc // GPC
                    bank = fc % 2
                    k = fc % 2
                    for ch in range(4):
                        mm(cx, cx.ps[bank][:, ch * 128:(ch + 1) * 128], vt[:, ch, fc * 128:(fc + 1) * 128],
                           wg[:, g * 128:(g + 1) * 128], True, True, [Bv[ch], Bwg], [cx.Bps[bank]])
                    S.add("dve", lambda e, fc=fc, bank=bank, k=k: e.scalar_tensor_tensor(
                        out=tmpm[k].rearrange("p (c t) -> p c t", c=4), in0=cx.ps[bank][:].rearrange("p (c t) -> p c t", c=4),
                        scalar=lng[:, fc:fc + 1],
                        in1=C2[:, fc * 128:(fc + 1) * 128].unsqueeze(1).to_broadcast([128, 4, 128]), op0=ALU.mult, op1=ALU.add),
                        reads=[cx.Bps[bank], Blng, BC2], writes=[Btmpm[k]])
                    S.add("dve", lambda e, fc=fc, k=k: e.tensor_tensor(out=uT[:, fc, :], in0=tmpm[k], in1=uT[:, fc, :], op=ALU.mult),
                          reads=[Btmpm[k], Bu[fc]], writes=[Bu[fc]])

            def out_piece(q, tg=tg, sl=sl):
                nonlocal ostep
                wb = ostep % 2
                ostep += 1
                if tg == 0 and not pre_done:
                    S.add("pool", lambda e: e.dma_start(out=wout[wb][:], in_=woutv[:, q * 4:(q + 1) * 4, :]),
                          writes=[Bwout[wb]], dma=True)
                    S.add("sp", lambda e: e.dma_start(out=wout_c[q].rearrange("q (c f) -> q c f", c=4), in_=wout[wb][:]),
                          reads=[Bwout[wb]], writes=[Bwout_c[q]], dma=True)
                else:
                    S.add("sp", lambda e: e.dma_start(out=wout[wb][:], in_=wout_c[q].rearrange("q (c f) -> q c f", c=4)),
                          reads=[Bwout_c[q]], writes=[Bwout[wb]], dma=True)
                for oc in range(DC):
                    bank = 4 + (oc % 3)
                    for j in range(4):
                        mm(cx, cx.ps[bank][:], wout[wb][:, j, oc * 128:(oc + 1) * 128], uT[:, q * 4 + j, :],
                           j == 0, j == 3, [Bwout[wb], Bu[q * 4 + j]], [cx.Bps[bank]])
                    S.add("dve", lambda e, oc=oc, bank=bank: e.tensor_tensor(
                        out=cx.xT[:, oc, sl], in0=cx.xT[:, oc, sl], in1=cx.ps[bank][:], op=ALU.add),
                        reads=[cx.Bps[bank], cx.Bx[tg]], writes=[cx.Bx[tg]])

            gate_piece(0)
            for q in range(6):
                if q + 1 < 6:
                    gate_piece(q + 1)
                out_piece(q)


def _dram_in(nc, name, shape, dt=F32):
    return nc.dram_tensor(name, list(shape), dt, kind="ExternalInput").ap()


def _dram_out(nc, name, shape, dt=F32):
    return nc.dram_tensor(name, list(shape), dt, kind="ExternalOutput").ap()


def odd_decl(nc, tag):
    d = _Scratch()
    d.tag = tag
    d.mixg = _dram_in(nc, tag + "mixg", [D])
    d.win = _dram_in(nc, tag + "gwin", [D, 2 * GF])
    d.lng = _dram_in(nc, tag + "lng", [GF])
    d.lnb = _dram_in(nc, tag + "lnb", [GF])
    d.wsT = _dram_in(nc, tag + "wsT", [GG, 128, 128])
    d.bs = _dram_in(nc, tag + "bs", [GG, 128])
    d.wout = _dram_in(nc, tag + "gwout", [GF, D])
    d.mlpg = _dram_in(nc, tag + "mlpg", [D])
    d.w1 = _dram_in(nc, tag + "w1", [D, HID])
    d.w2 = _dram_in(nc, tag + "w2", [HID, D])
    return d


def odd_run(cx, d, pre=None):
    tag = d.tag
    gT, Bg = cx.vec(tag + "mixg_sb", d.mixg, DC)
    lngT, Blng = cx.vec(tag + "lng_sb", d.lng, GFC)
    lnbT, Blnb = cx.vec(tag + "lnb_sb", d.lnb, GFC)
    mgT, Bmg = cx.vec(tag + "mlpg_sb", d.mlpg, DC)
    gmlp_phase(cx, gT, Bg, d.win, lngT, Blng, lnbT, Blnb, d.wsT, d.bs, d.wout, pre=pre)
    mlp_layer(cx, mgT, Bmg, d.w1, d.w2)


def odd_layer(cx, nc, tag):
    odd_run(cx, odd_decl(nc, tag))


def build_test_odd():
    nc = bass.Bass("TRN2", target_bir_lowering=False)
    xin = _dram_in(nc, "xT_in", [D, TOK])
    xout = _dram_out(nc, "xT_out", [D, TOK])
    cx = Ctx(nc)
    load_xT(cx, xin)
    odd_layer(cx, nc, "o_")
    with Phase(cx):
        store_T(cx, xout, cx.xT, cx.Bx)
    return nc


WSEL = 7 * 2 * 128
NT8 = SEQ // TT
NBLK = SEQ // 128
NCH = SEQ // 64
MID = 31
SB_SCALE = 128 ** -0.5


def _col(kind, h):
    return (kind if kind < 4 else kind - 4) * 128


def _head_cols(is_b, h):
    return (1024 + h * 384, 384) if is_b else (h * 512, 512)


class P2Ctx:
    pass


def p2_consts(cx, nc):
    S = cx.S
    p = P2Ctx()
    p.hstep = 0
    p.lb = nc.alloc_sbuf_tensor("lb", [128, 2], F32)
    p.oml = nc.alloc_sbuf_tensor("oml", [128, 2], F32)
    p.Blb = Buf()

    def head_bufs(ph, col0, ncols, new_ring=True):
        w = ph.sb("wh", [128, DC, ncols], BF16)
        Bw = Buf()
        S.add("pool", lambda e: e.dma_start(out=w[:], in_=p.wv[:, :, col0:col0 + ncols]), writes=[Bw], dma=True)
        p.w, p.Bw = w, Bw
        if new_ring:
            p.hring = [ph.sb("hring%d" % i, [128, DC, TT], BF16) for i in range(2)]
            p.Bhring = [Buf(), Buf()]
    p.head_bufs = head_bufs

    def load_h(t):
        k = p.hstep % 2
        p.hstep += 1
        src = p.h_tile(t)
        ring, B = p.hring[k], p.Bhring[k]
        S.add("sp", lambda e: e.dma_start(out=ring[:], in_=src), reads=[p.Bhsrc], writes=[B], dma=True)
        return ring, B
    p.load_h = load_h
    def mask(name, dt, pattern, cm, op, base_val, fill):
        t = nc.alloc_sbuf_tensor(name, [128, 128], dt)
        B = Buf()
        S.add("pool", lambda e: e.memset(t[:], base_val), writes=[B])
        S.add("pool", lambda e: e.affine_select(out=t[:], in_=t[:], pattern=pattern, compare_op=op, fill=fill,
                                                base=0, channel_multiplier=cm), reads=[B], writes=[B])
        return t, B
    p.ident, p.Bident = mask("ident", BF16, [[1, 128]], -1, ALU.is_equal, 1.0, 0.0)
    p.bdm, p.Bbdm = mask("bdm", F32, [[1, 128]], -1, ALU.is_ge, 1.0, 0.0)
    S.add("pool", lambda e: e.memset(p.bdm[0:64, 64:128], 0.0), reads=[p.Bbdm], writes=[p.Bbdm])
    p.U, p.BU = mask("Umask", BF16, [[-1, 128]], 1, ALU.is_gt, 1.0, 0.0)
    p.V, p.BV = mask("Vmask", BF16, [[1, 128]], -1, ALU.is_ge, 1.0, 0.0)
    p.Lm, p.BLm = mask("Lmask", F32, [[1, 128]], -1, ALU.is_gt, 1.0, 0.0)
    p.mb, p.Bmb = mask("mbias", F32, [[1, 128]], -1, ALU.is_gt, 0.0, -30000.0)
    p.lom = nc.alloc_sbuf_tensor("lom", [128, 2], F32)
    p.Blom = Buf()
    S.add("pool", lambda e: e.memset(p.lom[:], 1.0), writes=[p.Blom])
    S.add("pool", lambda e: e.affine_select(out=p.lom[:, 0:1], in_=p.lom[:, 0:1], pattern=[[0, 1]], compare_op=ALU.is_ge,
                                            fill=0.0, base=63, channel_multiplier=-1), reads=[p.Blom], writes=[p.Blom])
    S.add("pool", lambda e: e.affine_select(out=p.lom[:, 1:2], in_=p.lom[:, 1:2], pattern=[[0, 1]], compare_op=ALU.is_ge,
                                            fill=0.0, base=-64, channel_multiplier=1), reads=[p.Blom], writes=[p.Blom])
    p.rmask = nc.alloc_sbuf_tensor("rmask", [128, TT], F32)
    p.Brmask = Buf()
    S.add("pool", lambda e: e.memset(p.rmask[:], 1.0), writes=[p.Brmask])
    S.add("pool", lambda e: e.memset(p.rmask[:].rearrange("p (c s) -> p c s", s=64)[:, :, 0:1], 0.0),
          reads=[p.Brmask], writes=[p.Brmask])
    return p


def p2_layer(cx, p, e_idx, tag, h_tile, Bhsrc, w_d, lbl_d, gn_d):
    S = cx.S
    p.h_tile, p.Bhsrc = h_tile, Bhsrc
    p.wv = w_d.rearrange("(c p) f -> p c f", p=128)
    p.gn, p.Bgn = cx.vec(tag + "gn_sb", gn_d, 2)
    if e_idx == 0:
        S.add("dve", lambda e: e.memset(p.lb[:], 0.0), writes=[p.Blb])
        S.add("dve", lambda e: e.memset(p.oml[:], 1.0), writes=[p.Blb])
    else:
        l0, Bl0 = cx.vec(tag + "l0_sb", lbl_d[0], 2)
        l1, Bl1 = cx.vec(tag + "l1_sb", lbl_d[1], 2)
        S.add("dve", lambda e: e.tensor_tensor(out=p.lb[:], in0=l1[:], in1=l0[:], op=ALU.subtract),
              reads=[Bl0, Bl1], writes=[p.Blb])
        S.add("act", lambda e: e.activation(out=p.lb[:], in_=p.lb[:], func=AF.Sigmoid), reads=[p.Blb], writes=[p.Blb])
        S.add("dve", lambda e: e.tensor_scalar(out=p.oml[:], in0=p.lb[:], scalar1=-1.0, scalar2=1.0,
                                               op0=ALU.mult, op1=ALU.add), reads=[p.Blb], writes=[p.Blb])


def _proj_fm(cx, p, bank, col0, ht, Bht):
    for kc in range(DC):
        mm(cx, cx.ps[bank][:], p.w[:, kc, col0:col0 + 128], ht[:, kc, :], kc == 0, kc == DC - 1,
           [p.Bw, Bht], [cx.Bps[bank]])


def _proj_tm(cx, p, bank, col0, ht, Bht):
    for b4 in range(4):
        for kc in range(DC):
            mm(cx, cx.ps[bank][:, b4 * 128:(b4 + 1) * 128], ht[:, kc, b4 * 128:(b4 + 1) * 128], p.w[:, kc, col0:col0 + 128],
               kc == 0, kc == DC - 1, [p.Bw, Bht], [cx.Bps[bank]])


def hgrn_head(cx, p, h, oT_d):
    S = cx.S
    with Phase(cx) as ph:
        qt = ph.sb("qt", [128, SEQ], BF16)
        kt = ph.sb("kt", [128, SEQ], BF16)
        Bqt = [Buf() for _ in range(NT8)]
        Bkt = [Buf() for _ in range(NT8)]
        ktlo = [ph.sb("ktlo%d" % i, [128, 128], BF16) for i in range(2)]
        kthi = [ph.sb("kthi%d" % i, [128, 128], BF16) for i in range(2)]
        Bktlo, Bkthi = [Buf(), Buf()], [Buf(), Buf()]
        vtok = ph.sb("vtok", [128, SEQ], BF16)
        Bvtok = [Buf() for _ in range(NT8)]
        gate = ph.sb("gate", [128, SEQ], BF16)
        Bgate = [Buf() for _ in range(NT8)]
        bl = ph.sb("bl", [128, NCH], F32)
        bm = ph.sb("bm", [128, NCH], F32)
        Bblm = Buf()
        c1 = ph.sb("c1", [128, NCH], F32)
        c2 = ph.sb("c2", [128, NCH], F32)
        cm = ph.sb("cm", [128, NCH], F32)
        Bc = Buf()
        def dbl(name, dt=F32, w=TT):
            return [ph.sb("%s%d" % (name, i), [128, w], dt) for i in range(2)], [Buf(), Buf()]
        qf, Bqf = dbl("qf")
        sg, Bsg = dbl("sg")
        lf, Blf = dbl("lf")
        kk, Bkk = dbl("kk")
        bb, Bbb = dbl("bb")
        bp, Bbp = dbl("bp")
        Ep, BEp = dbl("Ep")
        Em, BEm = dbl("Em")
        gf, Bgf = dbl("gf")
        p.head_bufs(ph, *_head_cols(False, h))
        for t in range(NT8):
            k = t % 2
            sl = slice(t * TT, (t + 1) * TT)
            ht, Bht = p.load_h(t)
            _proj_fm(cx, p, 0, _col(0, h), ht, Bht)
            S.add("act", lambda e, k=k: e.activation(out=qf[k][:], in_=cx.ps[0][:], func=AF.Silu),
                  reads=[cx.Bps[0]], writes=[Bqf[k]])
            _proj_fm(cx, p, 1, _col(1, h), ht, Bht)
            S.add("act", lambda e, k=k: e.activation(out=sg[k][:], in_=cx.ps[1][:], func=AF.Sigmoid),
                  reads=[cx.Bps[1]], writes=[Bsg[k]])
            S.add("dve", lambda e, k=k: e.tensor_scalar(out=sg[k][:], in0=sg[k][:], scalar1=p.oml[:, h:h + 1],
                                                        scalar2=p.lb[:, h:h + 1], op0=ALU.mult, op1=ALU.add),
                  reads=[Bsg[k], p.Blb], writes=[Bsg[k]])
            S.add("act", lambda e, k=k: e.activation(out=lf[k][:], in_=sg[k][:], func=AF.Ln),
                  reads=[Bsg[k]], writes=[Blf[k]])
            S.add("dve", lambda e, k=k: e.tensor_scalar(out=kk[k][:], in0=sg[k][:], scalar1=-1.0, scalar2=1.0,
                                                        op0=ALU.mult, op1=ALU.add), reads=[Bsg[k]], writes=[Bkk[k]])
            S.add("dve", lambda e, k=k: e.tensor_tensor_scan(out=bb[k][:], data0=p.rmask[:], data1=lf[k][:], initial=0.0,
                                                             op0=ALU.mult, op1=ALU.add),
                  reads=[Blf[k], p.Brmask], writes=[Bbb[k]])
            b3 = lambda k: bb[k][:].rearrange("p (c s) -> p c s", s=64)
            S.add("dve", lambda e, k=k: e.tensor_tensor(out=bp[k][:].rearrange("p (c s) -> p c s", s=64), in0=b3(k),
                                                        in1=b3(k)[:, :, MID:MID + 1].to_broadcast([128, 8, 64]),
                                                        op=ALU.subtract), reads=[Bbb[k]], writes=[Bbp[k]])
            S.add("dve", lambda e, k=k, t=t: e.tensor_copy(out=bl[:, t * 8:(t + 1) * 8], in_=b3(k)[:, :, 63]),
                  reads=[Bbb[k]], writes=[Bblm])
            S.add("dve", lambda e, k=k, t=t: e.tensor_copy(out=bm[:, t * 8:(t + 1) * 8], in_=b3(k)[:, :, MID]),
                  reads=[Bbb[k]], writes=[Bblm])
            S.add("act", lambda e, k=k: e.activation(out=Ep[k][:], in_=bp[k][:], func=AF.Exp), reads=[Bbp[k]], writes=[BEp[k]])
            S.add("act", lambda e, k=k: e.activation(out=Em[k][:], in_=bp[k][:], func=AF.Exp, scale=-1.0),
                  reads=[Bbp[k]], writes=[BEm[k]])
            S.add("dve", lambda e, k=k, sl=sl: e.tensor_tensor(out=qt[:, sl], in0=qf[k][:], in1=Ep[k][:], op=ALU.mult),
                  reads=[Bqf[k], BEp[k]], writes=[Bqt[t]])
            S.add("dve", lambda e, k=k, sl=sl: e.tensor_tensor(out=kt[:, sl], in0=kk[k][:], in1=Em[k][:], op=ALU.mult),
                  reads=[Bkk[k], BEm[k]], writes=[Bkt[t]])
            _proj_fm(cx, p, 2, _col(3, h), ht, Bht)
            S.add("act", lambda e, k=k: e.activation(out=gf[k][:], in_=cx.ps[2][:], func=AF.Silu),
                  reads=[cx.Bps[2]], writes=[Bgf[k]])
            S.add("dve", lambda e, k=k, sl=sl: e.tensor_scalar(out=gate[:, sl], in0=gf[k][:], scalar1=p.gn[:, h:h + 1],
                                                               scalar2=None, op0=ALU.mult),
                  reads=[Bgf[k], p.Bgn], writes=[Bgate[t]])
            _proj_tm(cx, p, 3, _col(2, h), ht, Bht)
            S.add("act", lambda e, sl=sl: e.activation(out=vtok[:, sl], in_=cx.ps[3][:], func=AF.Copy),
                  reads=[cx.Bps[3]], writes=[Bvtok[t]])
        S.add("act", lambda e: e.activation(out=c1[:], in_=bl[:], func=AF.Exp), reads=[Bblm], writes=[Bc])
        S.add("act", lambda e: e.activation(out=cm[:], in_=bm[:], func=AF.Exp), reads=[Bblm], writes=[Bc])
        S.add("dve", lambda e: e.tensor_tensor(out=c2[:], in0=bl[:], in1=bm[:], op=ALU.subtract), reads=[Bblm], writes=[Bc])
        S.add("act", lambda e: e.activation(out=c2[:], in_=c2[:], func=AF.Exp), reads=[Bc], writes=[Bc])
        if "nostep3" in P2_PARTS:
            return
        Sst = ph.sb("Sst", [128, 128], F32)
        BSst = Buf()
        St = [ph.sb("St%d" % i, [128, 128], BF16) for i in range(2)]
        BSt = [Buf(), Buf()]
        Xs = [ph.sb("Xs%d" % i, [128, 128], F32) for i in range(2)]
        BXs = [Buf(), Buf()]
        S.add("dve", lambda e: e.memset(Sst[:], 0.0), writes=[BSst])
        S.add("dve", lambda e: e.memset(St[0][:], 0.0), writes=[BSt[0]])
        scm, Bscm = dbl("scm", BF16, 128)
        sqf, Bsqf = dbl("sqf")
        rr, Brr = dbl("rr")
        o1, Bo1 = dbl("o1")
        ost, Bost = dbl("ost", BF16)
        Bout = Buf()
        def stage_a(i):
            t = i // 4
            k = i % 2
            cs = slice(i * 128, (i + 1) * 128)
            mm(cx, cx.ps[6][:, k * 128:(k + 1) * 128], kt[:, cs], p.ident[:], True, True, [Bkt[t], p.Bident], [cx.Bps[6]])
            S.add("act", lambda e, k=k: e.activation(out=ktlo[k][:], in_=cx.ps[6][:, k * 128:(k + 1) * 128], func=AF.Copy,
                                                     scale=p.lom[:, 0:1]), reads=[cx.Bps[6], p.Blom], writes=[Bktlo[k]])
            S.add("act", lambda e, k=k: e.activation(out=kthi[k][:], in_=cx.ps[6][:, k * 128:(k + 1) * 128], func=AF.Copy,
                                                     scale=p.lom[:, 1:2]), reads=[cx.Bps[6], p.Blom], writes=[Bkthi[k]])
            sb_ = i % 2
            mm(cx, cx.ps[sb_][:, 0:128], kt[:, cs], qt[:, cs], True, True, [Bkt[t], Bqt[t]], [cx.Bps[sb_]])
            S.add("dve", lambda e, k=k, sb_=sb_: e.tensor_tensor(out=scm[k][:], in0=cx.ps[sb_][:, 0:128], in1=p.bdm[:], op=ALU.mult),
                  reads=[cx.Bps[sb_], p.Bbdm], writes=[Bscm[k]])
            xb = 2 + i % 2
            mm(cx, cx.ps[xb][:, 0:128], ktlo[k][:], vtok[:, cs], True, True, [Bktlo[k], Bvtok[t]], [cx.Bps[xb]])
            mm(cx, cx.ps[xb][:, 128:256], kthi[k][:], vtok[:, cs], True, True, [Bkthi[k], Bvtok[t]], [cx.Bps[xb]])

        def stage_b(i):
            t = i // 4
            k = i % 2
            cs = slice(i * 128, (i + 1) * 128)
            xb = 2 + i % 2
            ob = 4 + (i // 4) % 2
            oc = (i % 4) * 128
            mm(cx, cx.ps[ob][:, oc:oc + 128], vtok[:, cs], scm[k][:], True, False, [Bvtok[t], Bscm[k]], [cx.Bps[ob]])
            for half in range(2):
                c = 2 * i + half
                sp_ = c % 2
                qs = slice(i * 128 + half * 64, i * 128 + (half + 1) * 64)
                mm(cx, cx.ps[ob][:, oc + half * 64:oc + (half + 1) * 64], St[sp_][:], qt[:, qs], False, half == 1,
                   [BSt[sp_], Bqt[t]], [cx.Bps[ob]])
                xk = c % 2
                S.add("act", lambda e, c=c, xb=xb, half=half, xk=xk: e.activation(
                    out=Xs[xk][:], in_=cx.ps[xb][:, half * 128:(half + 1) * 128], func=AF.Copy, scale=c2[:, c:c + 1]),
                    reads=[cx.Bps[xb], Bc], writes=[BXs[xk]])
                S.add("dve", lambda e, c=c, xk=xk: e.scalar_tensor_tensor(
                    out=Sst[:], in0=Sst[:], scalar=c1[:, c:c + 1], in1=Xs[xk][:], op0=ALU.mult, op1=ALU.add),
                    reads=[BSst, BXs[xk], Bc], writes=[BSst])
                if c + 1 < NCH:
                    S.add("dve", lambda e, c=c, sp_=sp_: e.tensor_scalar(out=St[1 - sp_][:], in0=Sst[:], scalar1=cm[:, c + 1:c + 2],
                                                                          scalar2=None, op0=ALU.mult),
                          reads=[BSst, Bc], writes=[BSt[1 - sp_]])
            if i % 4 == 3:
                k2 = t % 2
                sl = slice(t * TT, (t + 1) * TT)
                S.add("act", lambda e, k2=k2, ob=ob: e.activation(out=sqf[k2][:], in_=cx.ps[ob][:], func=AF.Square),
                      reads=[cx.Bps[ob]], writes=[Bsqf[k2]])
                mm(cx, cx.ps[6][:], cx.ones[:], sqf[k2][:], True, True, [Bsqf[k2], cx.Bones], [cx.Bps[6]])
                S.add("act", lambda e, k2=k2: e.activation(out=rr[k2][:], in_=cx.ps[6][:], func=AF.Ln, scale=1.0 / 128, bias=cx.epsb[:]),
                      reads=[cx.Bps[6], cx.Beps], writes=[Brr[k2]])
                S.add("act", lambda e, k2=k2: e.activation(out=rr[k2][:], in_=rr[k2][:], func=AF.Exp, scale=-0.5),
                      reads=[Brr[k2]], writes=[Brr[k2]])
                S.add("dve", lambda e, k2=k2, ob=ob: e.tensor_tensor(out=o1[k2][:], in0=cx.ps[ob][:], in1=rr[k2][:], op=ALU.mult),
                      reads=[cx.Bps[ob], Brr[k2]], writes=[Bo1[k2]])
                S.add("dve", lambda e, k2=k2, sl=sl: e.tensor_tensor(out=ost[k2][:], in0=o1[k2][:], in1=gate[:, sl], op=ALU.mult),
                      reads=[Bo1[k2], Bgate[t]], writes=[Bost[k2]])
                dst = p.o_dst(h * 128, t)
                S.add("sp", lambda e, k2=k2, dst=dst: e.dma_start(out=dst, in_=ost[k2][:]),
                      reads=[Bost[k2]], writes=[Bout], dma=True)

        stage_a(0)
        for i in range(NBLK):
            if i + 1 < NBLK:
                stage_a(i + 1)
            stage_b(i)


def sb_heads(cx, p, heads, pre=None):
    S = cx.S
    with Phase(cx) as ph:
        if pre is not None:
            pre()
        def ring(name, n, dt=F32, w=TT):
            return [ph.sb("%s%d" % (name, i), [128, w], dt) for i in range(n)], [Buf() for _ in range(n)]
        zer = ph.sb("zer", [128, 128], BF16)
        Bzer = Buf()
        S.add("dve", lambda e: e.memset(zer[:], 0.0), writes=[Bzer])
        steps = [(G, J) for G in range(NT8) for J in range(4 * G + 3, -1, -1)]
        N = len(steps)

        def geom(n):
            G, J = steps[n]
            lo = max(J - 4 * G, 0) * 128
            return G, J, lo, J >= 4 * G, slice(lo, TT), slice(J * 128, (J + 1) * 128)

        NH = len(heads)
        sp_all = [ph.sb("spall%d" % i, [128, NH * TT], F32) for i in range(3)]
        tm_all = [ph.sb("tmall%d" % i, [128, NH * TT], F32) for i in range(2)]
        at_all = [ph.sb("atall%d" % i, [128, NH * TT], BF16) for i in range(2)]

        def both(t, cs):
            return t[:].rearrange("p (h t) -> p h t", h=NH)[:, :, cs]
        chains = []
        for ci, h in enumerate(heads):
            c = P2Ctx()
            c.h = h
            c.b0 = 4 * ci
            c.qT = ph.sb("qT%d" % ci, [128, SEQ], BF16)
            c.kT = ph.sb("kT%d" % ci, [128, SEQ], BF16)
            c.vtok = ph.sb("vtokb%d" % ci, [128, SEQ], BF16)
            c.Bq = [Buf() for _ in range(NT8)]
            c.Bk = [Buf() for _ in range(NT8)]
            c.Bv = [Buf() for _ in range(NT8)]
            hs = slice(ci * TT, (ci + 1) * TT)
            c.sp, c.Bsp = [t[:, hs] for t in sp_all], [Buf() for _ in range(3)]
            c.Lb, c.BLb = ring("Lb%d_" % ci, 3, BF16)
            c.tm, c.Btm = [t[:, hs] for t in tm_all], [Buf() for _ in range(2)]
            c.at, c.Bat = [t[:, hs] for t in at_all], [Buf() for _ in range(2)]
            c.ost, c.Bost = ring("ostb%d_" % ci, 2, BF16)
            c.Bout = Buf()
            chains.append(c)
        for ci, c in enumerate(chains):
            p.head_bufs(ph, *_head_cols(True, c.h), new_ring=(ci == 0))
            for t in range(NT8):
                sl = slice(t * TT, (t + 1) * TT)
                ht, Bht = p.load_h(t)
                _proj_fm(cx, p, c.b0, _col(4, c.h), ht, Bht)
                S.add("act", lambda e, sl=sl, c=c: e.activation(out=c.qT[:, sl], in_=cx.ps[c.b0][:], func=AF.Copy),
                      reads=[cx.Bps[c.b0]], writes=[c.Bq[t]])
                _proj_fm(cx, p, c.b0 + 1, _col(5, c.h), ht, Bht)
                S.add("dve", lambda e, sl=sl, c=c: e.tensor_copy(out=c.kT[:, sl], in_=cx.ps[c.b0 + 1][:]),
                      reads=[cx.Bps[c.b0 + 1]], writes=[c.Bk[t]])
                _proj_tm(cx, p, c.b0 + 2, _col(6, c.h), ht, Bht)
                S.add("act", lambda e, sl=sl, c=c: e.activation(out=c.vtok[:, sl], in_=cx.ps[c.b0 + 2][:], func=AF.Copy),
                      reads=[cx.Bps[c.b0 + 2]], writes=[c.Bv[t]])

        def a_pe(c, n):
            G, J, lo, diag, cs, js = geom(n)
            zb = c.b0 + n % 2
            mm(cx, cx.ps[zb][:, cs], c.kT[:, js], c.qT[:, G * TT + lo:(G + 1) * TT], True, True, [c.Bk[J // 4], c.Bq[G]], [cx.Bps[zb]])

        def a_exp(c, n):
            G, J, lo, diag, cs, js = geom(n)
            zb, k = c.b0 + n % 2, n % 3
            S.add("act", lambda e: e.activation(out=c.sp[k][:, cs], in_=cx.ps[zb][:, cs], func=AF.Exp, scale=-SB_SCALE),
                  reads=[cx.Bps[zb]], writes=[c.Bsp[k]])

        def a_ln(n):
            G, J, lo, diag, cs, js = geom(n)
            k = n % 3
            Bs = [c.Bsp[k] for c in chains]
            S.add("act", lambda e: e.activation(out=both(sp_all[k], cs), in_=both(sp_all[k], cs), func=AF.Ln, bias=cx.oneb[:]),
                  reads=Bs + [cx.Beps], writes=Bs)

        def a_dve(c, n):
            G, J, lo, diag, cs, js = geom(n)
            zb, k = c.b0 + n % 2, n % 3
            S.add("dve", lambda e: e.scalar_tensor_tensor(out=c.Lb[k][:, cs], in0=cx.ps[zb][:, cs], scalar=-SB_SCALE,
                                                          in1=c.sp[k][:, cs], op0=ALU.mult, op1=ALU.subtract),
                  reads=[cx.Bps[zb], c.Bsp[k]], writes=[c.BLb[k]])
            if diag:
                ds_ = slice(lo, lo + 128)
                S.add("dve", lambda e: e.tensor_tensor(out=c.Lb[k][:, ds_], in0=c.Lb[k][:, ds_], in1=p.Lm[:], op=ALU.mult),
                      reads=[c.BLb[k], p.BLm], writes=[c.BLb[k]])

        def b_open(c, n):
            G, J, lo, diag, cs, js = geom(n)
            if J == 4 * G + 3:
                for b in (c.b0 + 2, c.b0 + 3):
                    mm(cx, cx.ps[b][:, :], zer[:], c.qT[:, G * TT:(G + 1) * TT], True, True, [Bzer, c.Bq[G]], [cx.Bps[b]], skip=True)

        def b_u(c, n):
            G, J, lo, diag, cs, js = geom(n)
            k, b = n % 3, c.b0 + 2
            mm(cx, cx.ps[b][:, cs], p.U[:], c.Lb[k][:, cs], False, True, [p.BU, c.BLb[k]], [cx.Bps[b]], skip=True)

        def b_tm(c, n):
            G, J, lo, diag, cs, js = geom(n)
            k, k2, b = n % 3, n % 2, c.b0 + 2
            S.add("dve", lambda e: e.tensor_tensor(out=c.tm[k2][:, cs], in0=cx.ps[b][:, cs], in1=c.sp[k][:, cs], op=ALU.subtract),
                  reads=[cx.Bps[b], c.Bsp[k]], writes=[c.Btm[k2]])
            if diag:
                ds_ = slice(lo, lo + 128)
                S.add("dve", lambda e: e.tensor_tensor(out=c.tm[k2][:, ds_], in0=c.tm[k2][:, ds_], in1=p.mb[:], op=ALU.add),
                      reads=[c.Btm[k2], p.Bmb], writes=[c.Btm[k2]])

        def b_exp(n):
            G, J, lo, diag, cs, js = geom(n)
            k2 = n % 2
            S.add("act", lambda e: e.activation(out=both(at_all[k2], cs), in_=both(tm_all[k2], cs), func=AF.Exp),
                  reads=[c.Btm[k2] for c in chains], writes=[c.Bat[k2] for c in chains])

        def b_v(c, n):
            G, J, lo, diag, cs, js = geom(n)
            k, b = n % 3, c.b0 + 2
            mm(cx, cx.ps[b][:, cs], p.V[:], c.Lb[k][:, cs], False, True, [p.BV, c.BLb[k]], [cx.Bps[b]], skip=True)

        def b_av(c, n):
            G, J, lo, diag, cs, js = geom(n)
            k2, b = n % 2, c.b0 + 3
            mm(cx, cx.ps[b][:, cs], c.vtok[:, js], c.at[k2][:, cs], False, True, [c.Bv[J // 4], c.Bat[k2]], [cx.Bps[b]], skip=True)
            if J == 0:
                ko = G % 2
                S.add("act", lambda e: e.activation(out=c.ost[ko][:], in_=cx.ps[b][:], func=AF.Copy), reads=[cx.Bps[b]], writes=[c.Bost[ko]])
                dst = p.o_dst(256 + c.h * 128, G)
                S.add("sp", lambda e: e.dma_start(out=dst, in_=c.ost[ko][:]), reads=[c.Bost[ko]], writes=[c.Bout], dma=True)

        def each(fn, n):
            for c in chains:
                fn(c, n)

        for n0 in range(min(2, N)):
            each(a_pe, n0)
            each(a_exp, n0)
            a_ln(n0)
            each(a_dve, n0)
        for n in range(N):
            G_, J_ = steps[n]
            first = (J_ == 4 * G_ + 3)
            if n >= 1 and first:
                each(b_av, n - 1)
            each(b_open, n)
            each(b_u, n)
            each(b_tm, n)
            if n >= 1 and not first:
                each(b_av, n - 1)
            if n + 2 < N:
                each(a_pe, n + 2)
                each(a_exp, n + 2)
                a_ln(n + 2)
            b_exp(n)
            each(b_v, n)
            if n + 2 < N:
                each(a_dve, n + 2)
        each(b_av, N - 1)


P2_PARTS = ("a0", "a1", "b0", "b1")


def build_p2(e_idx):
    nc = bass.Bass("TRN2", target_bir_lowering=False)
    hT_d = _dram_in(nc, "hT_full", [D, SEQ], BF16)
    w_d = _dram_in(nc, "w_in", [D, WSEL])
    lbl_d = _dram_in(nc, "lbl", [2, 256])
    gn_d = _dram_in(nc, "gn", [256])
    oT_d = _dram_out(nc, "oT", [512, SEQ], BF16)
    cx = Ctx(nc, with_x=False)
    p = p2_consts(cx, nc)
    hv = hT_d.rearrange("(c p) n -> p c n", p=128)
    p2_layer(cx, p, e_idx, "", lambda t: hv[:, :, t * TT:(t + 1) * TT], Buf(), w_d, lbl_d, gn_d)
    p.o_dst = lambda row0, t: oT_d[row0:row0 + 128, t * TT:(t + 1) * TT]
    for h in range(2):
        if "a%d" % h in P2_PARTS:
            hgrn_head(cx, p, h, oT_d)
    bh = [h for h in range(2) if "b%d" % h in P2_PARTS]
    if bh:
        sb_heads(cx, p, bh)
    with Phase(cx):
        pass
    return nc


def build_p1():
    nc = bass.Bass("TRN2", target_bir_lowering=False)
    xin = _dram_in(nc, "xT_in", [D, TOK])
    g = _dram_in(nc, "mixg", [D])
    hout = _dram_out(nc, "hT", [D, TOK], BF16)
    cx = Ctx(nc)
    gT, Bg = cx.vec("mixg_sb", g, DC)
    load_xT(cx, xin)
    p1_phase(cx, gT, Bg, hout)
    return nc


def build_mid(final):
    nc = bass.Bass("TRN2", target_bir_lowering=False)
    xin = _dram_in(nc, "xT_in", [D, TOK])
    oT = _dram_in(nc, "oT_in", [D, TOK], BF16)
    wout = _dram_in(nc, "ab_wout", [D, D])
    mlpg = _dram_in(nc, "e_mlpg", [D])
    w1 = _dram_in(nc, "e_w1", [D, HID])
    w2 = _dram_in(nc, "e_w2", [HID, D])
    ng = _dram_in(nc, "next_g", [D])
    cx = Ctx(nc)
    mgT, Bmg = cx.vec("e_mlpg_sb", mlpg, DC)
    ngT, Bng = cx.vec("next_g_sb", ng, DC)
    load_xT(cx, xin)
    p3_phase(cx, oT, wout)
    mlp_layer(cx, mgT, Bmg, w1, w2)
    odd_layer(cx, nc, "o_")
    if final:
        out = _dram_out(nc, "outT", [D, TOK])
        final_phase(cx, ngT, Bng, out)
    else:
        xout = _dram_out(nc, "xT_out", [D, TOK])
        hout = _dram_out(nc, "hT", [D, TOK], BF16)
        with Phase(cx):
            store_T(cx, xout, cx.xT, cx.Bx)
        p1_phase(cx, ngT, Bng, hout)
    return nc


def p3_fused(cx, o_all, wout_dram, selT, Bsel):
    S = cx.S
    with Phase(cx) as ph:
        oA = ph.sb("oA", [128, DC, TOK], BF16)
        Bo = [Buf() for _ in range(NTT)]
        oB = [ph.sb("oB%d" % i, [128, DC, TT], BF16) for i in range(2)]
        BoB = [Buf(), Buf()]
        w = ph.sb("wout", [128, DC, D], BF16)
        Bw = Buf()
        ovA = o_all[0].rearrange("(c p) n -> p c n", p=128)
        ovB = o_all[1].rearrange("(c p) n -> p c n", p=128)
        S.add("pool", lambda e: e.dma_start(out=w[:], in_=wout_dram.rearrange("(c p) o -> p c o", p=128)),
              writes=[Bw], dma=True)
        for t in range(NTT):
            sl = slice(t * TT, (t + 1) * TT)
            sl1 = slice(TOK + t * TT, TOK + (t + 1) * TT)
            k = t % 2
            for half, ov in ((0, ovA), (1, ovB)):
                cs4 = slice(half * 4, half * 4 + 4)
                S.add("sp", lambda e, sl=sl, ov=ov, cs4=cs4: e.dma_start(out=oA[:, cs4, sl], in_=ov[:, :, sl]), writes=[Bo[t]], dma=True)
                S.add("sp", lambda e, sl1=sl1, ov=ov, cs4=cs4, k=k: e.dma_start(out=oB[k][:, cs4, :], in_=ov[:, :, sl1]),
                      writes=[BoB[k]], dma=True)
            for c in range(DC):
                S.add("dve", lambda e, c=c, sl=sl: e.tensor_scalar(out=oA[:, c, sl], in0=oA[:, c, sl], scalar1=selT[:, 0:1],
                                                                   scalar2=None, op0=ALU.mult), reads=[Bo[t], Bsel], writes=[Bo[t]])
                S.add("dve", lambda e, c=c, sl=sl, k=k: e.scalar_tensor_tensor(
                    out=oA[:, c, sl], in0=oB[k][:, c, :], scalar=selT[:, 1:2], in1=oA[:, c, sl], op0=ALU.mult, op1=ALU.add),
                    reads=[BoB[k], Bo[t], Bsel], writes=[Bo[t]])
            for oc in range(DC):
                bank = oc % 4
                for kc in range(DC):
                    mm(cx, cx.ps[bank][:], w[:, kc, oc * 128:(oc + 1) * 128], oA[:, kc, sl], kc == 0, kc == DC - 1,
                       [Bw, Bo[t]], [cx.Bps[bank]])
                S.add("dve", lambda e, oc=oc, sl=sl, bank=bank: e.tensor_tensor(
                    out=cx.xT[:, oc, sl], in0=cx.xT[:, oc, sl], in1=cx.ps[bank][:], op=ALU.add),
                    reads=[cx.Bps[bank], cx.Bx[t]], writes=[cx.Bx[t]])


PAIRS = [[0, 1], [2, 3], [4, 5], [6, 7]]


def build_fused():
    nc = bass.Bass("TRN2", target_bir_lowering=False)
    xin = _dram_in(nc, "xT_in", [D, TOK])
    sel_d = _dram_in(nc, "sel", [128, 2])
    out = _dram_out(nc, "outT", [D, TOK])
    fg = _dram_in(nc, "final_g", [D])
    HT = TOK // 2
    h_src = [nc.dram_tensor("h_src%d" % k, [D, HT], BF16) for k in range(2)]
    h_all = [nc.dram_tensor("h_all%d" % k, [2 * D, HT], BF16) for k in range(2)]
    o_src = [nc.dram_tensor("o_src%d" % k, [256, SEQ], BF16) for k in range(2)]
    o_all = [nc.dram_tensor("o_all%d" % k, [512, SEQ], BF16) for k in range(2)]
    cx = Ctx(nc)
    S = cx.S
    p = p2_consts(cx, nc)
    selT = nc.alloc_sbuf_tensor("sel_sb", [128, 2], F32)
    Bsel = Buf()
    S.add("sp", lambda e: e.dma_start(out=selT[:], in_=sel_d), writes=[Bsel], dma=True)
    fgT, Bfg = cx.vec("final_g_sb", fg, DC)
    load_xT(cx, xin)
    def h_dst(t):
        return h_src[t // 2].ap()[:, (t % 2) * TT:(t % 2 + 1) * TT].rearrange("(c p) n -> p c n", p=128)

    def h_tile(t):
        r, q = t // 4, t % 4
        return h_all[q // 2].ap()[r * D:(r + 1) * D, (q % 2) * TT:(q % 2 + 1) * TT].rearrange("(c p) n -> p c n", p=128)

    def o_dst(row0, t):
        return o_src[row0 // 256].ap()[row0 % 256:row0 % 256 + 128, t * TT:(t + 1) * TT]
    p.o_dst = o_dst

    def gather_one(src, dst):
        S.add("pool", lambda e_: e_.collective_compute("AllGather", ALU.bypass, replica_groups=PAIRS,
                                                       ins=[src.ap().opt()], outs=[dst.ap().opt()]), cc=True)

    def gather(src, dst):
        with Phase(cx):
            for k in range(2):
                S.add("pool", lambda e_, k=k: e_.collective_compute("AllGather", ALU.bypass, replica_groups=PAIRS,
                                                                    ins=[src[k].ap().opt()], outs=[dst[k].ap().opt()]), cc=True)

    for e in range(2):
        tag = "e%d_" % e
        mixg = _dram_in(nc, tag + "mixg", [D])
        w_in = _dram_in(nc, tag + "w_in", [D, WSEL])
        lbl = _dram_in(nc, tag + "lbl", [2, 256])
        gn = _dram_in(nc, tag + "gn", [256])
        wout = _dram_in(nc, tag + "wout", [D, D])
        mlpg = _dram_in(nc, tag + "mlpg", [D])
        w1 = _dram_in(nc, tag + "w1", [D, HID])
        w2 = _dram_in(nc, tag + "w2", [HID, D])
        gT, Bg = cx.vec(tag + "mixg_sb", mixg, DC)
        mgT, Bmg = cx.vec(tag + "mlpg_sb", mlpg, DC)
        p1_phase(cx, gT, Bg, None, dst_fn=h_dst)
        gather(h_src, h_all)
        p2_layer(cx, p, e, tag, h_tile, Buf(), w_in, lbl, gn)
        for h in range(2):
            hgrn_head(cx, p, h, None)
        sb_heads(cx, p, [0, 1], pre=lambda: gather_one(o_src[0], o_all[0]))
        with Phase(cx):
            gather_one(o_src[1], o_all[1])
        p3_fused(cx, [o_all[0].ap(), o_all[1].ap()], wout, selT, Bsel)
        od = odd_decl(nc, "o%d_" % e)
        sc = gmlp_scratch(cx)
        mlp_layer(cx, mgT, Bmg, w1, w2, extra=gmlp_precast_ops(cx, sc, od.win, od.wout))
        odd_run(cx, od, pre=sc)
    final_phase(cx, fgT, Bfg, out)
    return nc


def _run(nc, in_maps):
    res = run_bass_kernel_spmd(nc, in_maps, core_ids=list(range(NCORES)))
    return res.results


def _c(a):
    return np.ascontiguousarray(a)


def kernel_unfused(x, mix_norm, mlp_norm, mlp_w1, mlp_w2, ab_w_in, ab_w_out, hgrn_lb_logits, hgrn_out_norm,
           gm_w_in, gm_ln_g, gm_ln_b, gm_ws, gm_bs, gm_w_out, final_norm):
    f32 = lambda a: np.asarray(a, dtype=np.float32)
    x, mix_norm, mlp_norm, mlp_w1, mlp_w2 = f32(x), f32(mix_norm), f32(mlp_norm), f32(mlp_w1), f32(mlp_w2)
    ab_w_in, ab_w_out, hgrn_lb_logits, hgrn_out_norm = f32(ab_w_in), f32(ab_w_out), f32(hgrn_lb_logits), f32(hgrn_out_norm)
    gm_w_in, gm_ln_g, gm_ln_b, gm_ws, gm_bs, gm_w_out, final_norm = (f32(gm_w_in), f32(gm_ln_g), f32(gm_ln_b), f32(gm_ws),
                                                                        f32(gm_bs), f32(gm_w_out), f32(final_norm))
    cores = range(NCORES)
    xT = [_c(x[c // 2, (c % 2) * TOK:(c % 2 + 1) * TOK].T) for c in cores]

    def wsel(e, g):
        cols = []
        for kinds in (range(0, 4), range(4, 7)):
            for h in range(2):
                for kind in kinds:
                    c0 = kind * 512 + (2 * g + h) * 128
                    cols.append(ab_w_in[e][:, c0:c0 + 128])
        return _c(np.concatenate(cols, axis=1))

    def mixers(e, hT):
        hfull = [_c(np.concatenate([hT[2 * b], hT[2 * b + 1]], axis=1)) for b in range(4)]
        ins = [{"hT_full": hfull[c // 2], "w_in": wsel(e, c % 2),
                "lbl": _c(hgrn_lb_logits[:, (c % 2) * 256:(c % 2 + 1) * 256]),
                "gn": _c(hgrn_out_norm[e][(c % 2) * 256:(c % 2 + 1) * 256])} for c in cores]
        r = _run(build_p2(e), ins)
        outs = []
        for c in cores:
            b, half = c // 2, c % 2
            o0, o1 = r[2 * b]["oT"], r[2 * b + 1]["oT"]
            full = np.concatenate([o0[0:256], o1[0:256], o0[256:512], o1[256:512]], axis=0)
            outs.append(_c(full[:, half * TOK:(half + 1) * TOK]))
        return outs

    def odd_inputs(o, L):
        return {"o_mixg": mix_norm[L], "o_gwin": gm_w_in[o], "o_lng": gm_ln_g[o], "o_lnb": gm_ln_b[o],
                "o_wsT": _c(gm_ws[o].transpose(0, 2, 1)), "o_bs": gm_bs[o], "o_gwout": gm_w_out[o],
                "o_mlpg": mlp_norm[L], "o_w1": mlp_w1[L], "o_w2": mlp_w2[L]}

    r0 = _run(build_p1(), [{"xT_in": xT[c], "mixg": mix_norm[0]} for c in cores])
    oT = mixers(0, [r0[c]["hT"] for c in cores])
    common = dict(odd_inputs(0, 1), ab_wout=ab_w_out[0], e_mlpg=mlp_norm[0], e_w1=mlp_w1[0], e_w2=mlp_w2[0], next_g=mix_norm[2])
    r2 = _run(build_mid(False), [dict(common, xT_in=xT[c], oT_in=oT[c]) for c in cores])
    oT = mixers(1, [r2[c]["hT"] for c in cores])
    common = dict(odd_inputs(1, 3), ab_wout=ab_w_out[1], e_mlpg=mlp_norm[2], e_w1=mlp_w1[2], e_w2=mlp_w2[2], next_g=final_norm)
    r4 = _run(build_mid(True), [dict(common, xT_in=_c(r2[c]["xT_out"]), oT_in=oT[c]) for c in cores])
    out = np.empty((4, SEQ, D), dtype=np.float32)
    for c in cores:
        out[c // 2, (c % 2) * TOK:(c % 2 + 1) * TOK] = r4[c]["outT"].T
    return out


def kernel(x, mix_norm, mlp_norm, mlp_w1, mlp_w2, ab_w_in, ab_w_out, hgrn_lb_logits, hgrn_out_norm,
           gm_w_in, gm_ln_g, gm_ln_b, gm_ws, gm_bs, gm_w_out, final_norm):
    f32 = lambda a: np.asarray(a, dtype=np.float32)
    x, mix_norm, mlp_norm, mlp_w1, mlp_w2 = f32(x), f32(mix_norm), f32(mlp_norm), f32(mlp_w1), f32(mlp_w2)
    ab_w_in, ab_w_out, hgrn_lb_logits, hgrn_out_norm = f32(ab_w_in), f32(ab_w_out), f32(hgrn_lb_logits), f32(hgrn_out_norm)
    gm_w_in, gm_ln_g, gm_ln_b, gm_ws, gm_bs, gm_w_out, final_norm = (f32(gm_w_in), f32(gm_ln_g), f32(gm_ln_b), f32(gm_ws),
                                                                        f32(gm_bs), f32(gm_w_out), f32(final_norm))
    cores = range(NCORES)

    def wsel(e, g):
        cols = []
        for kinds in (range(0, 4), range(4, 7)):
            for h in range(2):
                for kind in kinds:
                    c0 = kind * 512 + (2 * g + h) * 128
                    cols.append(ab_w_in[e][:, c0:c0 + 128])
        return _c(np.concatenate(cols, axis=1))

    shared = {"final_g": final_norm}
    for e in range(2):
        L, tag = 2 * e, "e%d_" % e
        shared.update({tag + "mixg": mix_norm[L], tag + "wout": ab_w_out[e], tag + "mlpg": mlp_norm[L],
                       tag + "w1": mlp_w1[L], tag + "w2": mlp_w2[L]})
        o, Lo, t2 = e, 2 * e + 1, "o%d_" % e
        shared.update({t2 + "mixg": mix_norm[Lo], t2 + "gwin": gm_w_in[o], t2 + "lng": gm_ln_g[o], t2 + "lnb": gm_ln_b[o],
                       t2 + "wsT": _c(gm_ws[o].transpose(0, 2, 1)), t2 + "bs": gm_bs[o], t2 + "gwout": gm_w_out[o],
                       t2 + "mlpg": mlp_norm[Lo], t2 + "w1": mlp_w1[Lo], t2 + "w2": mlp_w2[Lo]})
    per_rank = []
    for g in range(2):
        d = {"sel": _c(np.tile(np.eye(2, dtype=np.float32)[g], (128, 1)))}
        for e in range(2):
            tag = "e%d_" % e
            d.update({tag + "w_in": wsel(e, g), tag + "lbl": _c(hgrn_lb_logits[:, g * 256:(g + 1) * 256]),
                      tag + "gn": _c(hgrn_out_norm[e][g * 256:(g + 1) * 256])})
        per_rank.append(d)
    in_maps = [dict(shared, **per_rank[c % 2], xT_in=_c(x[c // 2, (c % 2) * TOK:(c % 2 + 1) * TOK].T)) for c in cores]
    r = _run(build_fused(), in_maps)
    out = np.empty((4, SEQ, D), dtype=np.float32)
    for c in cores:
        out[c // 2, (c % 2) * TOK:(c % 2 + 1) * TOK] = r[c]["outT"].T
    return out
```
